# Optimizing a Trainium2 kernel written in Bass

```python
import jax, jax.numpy as jnp
from jax import lax
import numpy as np

D_MODEL = 1024
BATCH = 8
SEQ = 8192
DEPTH = 1
DEC_BATCH = 128
DEC_SEQ = 8
PAST_LEN = 8192
PAGE_SIZE = 128

WINDOWS = (128, 512, 2048)
DILATIONS = (1, 4, 16)
N_GROUPS = 3
HEADS_PER_GROUP = 4
HEAD_DIM = D_MODEL // 16
GROUP_WIDTH = HEADS_PER_GROUP * HEAD_DIM
ATTN_WIDTH = N_GROUPS * GROUP_WIDTH
CONV_CH = D_MODEL - ATTN_WIDTH
MIX_WIDTH = ATTN_WIDTH + CONV_CH
CONV_K = 3
IN_WIDTH = 3 * ATTN_WIDTH + 3 * CONV_CH
SPLITS = (ATTN_WIDTH, 2 * ATTN_WIDTH, 3 * ATTN_WIDTH,
          3 * ATTN_WIDTH + CONV_CH, 3 * ATTN_WIDTH + 2 * CONV_CH)
D_FF = 4 * D_MODEL
QBLOCK = 128
EPS = 1e-6

kernel_name = "hybrid_dilated_swa_shortconv_decode_step"


def rmsnorm(x, g):
    xf = x.astype(jnp.float32)
    y = xf * lax.rsqrt(jnp.mean(xf * xf, axis=-1, keepdims=True) + EPS)
    return (y * g.astype(jnp.float32)).astype(x.dtype)


def dilated_group_attn(q, k_ctx, v_ctx, q_idx, dilation, n_keys):
    idx = q_idx[:, None] - dilation * jnp.arange(n_keys, dtype=jnp.int32)[None, :]
    valid = idx >= 0
    idx = jnp.maximum(idx, 0)
    k_g = jnp.take(k_ctx, idx, axis=1)
    v_g = jnp.take(v_ctx, idx, axis=1)
    s = jnp.einsum("bthd,btkhd->bhtk", q, k_g).astype(jnp.float32) * (HEAD_DIM ** -0.5)
    s = jnp.where(valid[None, None], s, -jnp.inf)
    lse = jax.nn.logsumexp(s, axis=-1)
    p = jnp.exp(s - lse[..., None])
    o = jnp.einsum("bhtk,btkhd->bthd", p.astype(v_ctx.dtype), v_g)
    return o, jnp.transpose(lse, (0, 2, 1))


def attend_groups(q, k_ctxs, v_ctxs, q_idxs):
    b, t = q.shape[0], q.shape[1]
    outs, lses = [], []
    for g in range(N_GROUPS):
        o, lse = dilated_group_attn(q[:, :, g], k_ctxs[g], v_ctxs[g], q_idxs[g],
                                    DILATIONS[g], WINDOWS[g] // DILATIONS[g] + 1)
        outs.append(o)
        lses.append(lse)
    o = jnp.stack(outs, axis=2)
    lse = jnp.stack(lses, axis=2)
    alpha = jax.nn.softmax(lse, axis=2)
    return (o * alpha[..., None].astype(o.dtype)).reshape(b, t, ATTN_WIDTH)


def mixing_sublayer(xn, kv_pasts, conv_past, w_in, conv_w, w_out, blocked):
    b, t, _ = xn.shape
    proj = jnp.einsum("btd,de->bte", xn, w_in)
    q, k, v, gate_b, gate_c, h = jnp.split(proj, SPLITS, axis=-1)
    q = q.reshape(b, t, N_GROUPS, HEADS_PER_GROUP, HEAD_DIM)
    k = k.reshape(b, t, N_GROUPS, HEADS_PER_GROUP, HEAD_DIM)
    v = v.reshape(b, t, N_GROUPS, HEADS_PER_GROUP, HEAD_DIM)

    k_ctxs, v_ctxs, q_idxs, new_kv = [], [], [], []
    for g in range(N_GROUPS):
        kv_new = jnp.stack([k[:, :, g], v[:, :, g]], axis=2)
        if kv_pasts is None:
            ctx = kv_new
            base = 0
        else:
            ctx = jnp.concatenate([kv_pasts[g], kv_new], axis=1)
            base = kv_pasts[g].shape[1]
        n_ctx = ctx.shape[1]
        k_ctxs.append(ctx[:, :, 0])
        v_ctxs.append(ctx[:, :, 1])
        q_idxs.append(base + jnp.arange(t, dtype=jnp.int32))
        new_kv.append(ctx[:, n_ctx - min(WINDOWS[g], n_ctx):])

    if blocked:
        def block_fn(start):
            qb = lax.dynamic_slice_in_dim(q, start, QBLOCK, axis=1)
            idx = start + jnp.arange(QBLOCK, dtype=jnp.int32)
            return attend_groups(qb, k_ctxs, v_ctxs, [idx] * N_GROUPS)
        starts = jnp.arange(t // QBLOCK, dtype=jnp.int32) * QBLOCK
        attn = lax.map(block_fn, starts)
        attn = jnp.transpose(attn, (1, 0, 2, 3)).reshape(b, t, ATTN_WIDTH)
    else:
        attn = attend_groups(q, k_ctxs, v_ctxs, q_idxs)

    u = gate_c * h
    past = jnp.zeros((b, CONV_K - 1, CONV_CH), u.dtype) if conv_past is None else conv_past
    cctx = jnp.concatenate([past, u], axis=1)
    y = conv_w[0] * cctx[:, 0:t]
    for j in range(1, CONV_K):
        y = y + conv_w[j] * cctx[:, j:j + t]
    conv_out = gate_b * y
    new_conv = cctx[:, cctx.shape[1] - (CONV_K - 1):]

    mixed = jnp.concatenate([attn, conv_out], axis=-1)
    return jnp.einsum("bte,ed->btd", mixed, w_out), new_kv, new_conv


def squared_relu_mlp(x, w_up, w_down):
    hdn = jnp.square(jax.nn.relu(jnp.einsum("btd,df->btf", x, w_up)))
    return jnp.einsum("btf,fd->btd", hdn, w_down)


def run_trunk(x, kv_caches, conv_state, norm_attn_g, w_in, conv_w, w_out,
              norm_mlp_g, w_up, w_down, norm_final_g, blocked):
    new_kv = [[], [], []]
    new_conv = []
    for l in range(DEPTH):
        kv_pasts = None if kv_caches is None else [c[l] for c in kv_caches]
        cpast = None if conv_state is None else conv_state[l]
        mix, kv_l, conv_l = mixing_sublayer(rmsnorm(x, norm_attn_g[l]), kv_pasts, cpast,
                                            w_in[l], conv_w[l], w_out[l], blocked)
        x = x + mix
        x = x + squared_relu_mlp(rmsnorm(x, norm_mlp_g[l]), w_up[l], w_down[l])
        for g in range(N_GROUPS):
            new_kv[g].append(kv_l[g])
        new_conv.append(conv_l)
    y = rmsnorm(x, norm_final_g)
    return (y, jnp.stack(new_kv[0]), jnp.stack(new_kv[1]), jnp.stack(new_kv[2]), jnp.stack(new_conv))


def setup_inputs(seed: int = 0) -> dict:
    key = jax.random.key(seed)
    ks = jax.random.split(key, 16)
    f32 = jnp.float32
    lens = [min(w, PAST_LEN) for w in WINDOWS]
    return {
        "x_prompt": jax.random.normal(ks[0], (BATCH, SEQ, D_MODEL), f32),
        "x_sample": jax.random.normal(ks[1], (DEC_BATCH, DEC_SEQ, D_MODEL), f32),
        "cache_kv_w128": jax.random.normal(ks[2], (DEPTH, DEC_BATCH, lens[0], 2, HEADS_PER_GROUP, HEAD_DIM), f32),
        "cache_kv_w512": jax.random.normal(ks[3], (DEPTH, DEC_BATCH, lens[1], 2, HEADS_PER_GROUP, HEAD_DIM), f32),
        "cache_kv_w2048": jax.random.normal(ks[4], (DEPTH, DEC_BATCH, lens[2], 2, HEADS_PER_GROUP, HEAD_DIM), f32),
        "state_conv": 0.5 * jax.random.normal(ks[5], (DEPTH, DEC_BATCH, CONV_K - 1, CONV_CH), f32),
        "norm_attn_g": 1.0 + 0.02 * jax.random.normal(ks[6], (DEPTH, D_MODEL), f32),
        "w_in": jax.random.normal(ks[7], (DEPTH, D_MODEL, IN_WIDTH), f32) * D_MODEL ** -0.5,
        "conv_w": jax.random.normal(ks[8], (DEPTH, CONV_K, CONV_CH), f32) * CONV_K ** -0.5,
        "w_out": jax.random.normal(ks[9], (DEPTH, MIX_WIDTH, D_MODEL), f32) * MIX_WIDTH ** -0.5,
        "norm_mlp_g": 1.0 + 0.02 * jax.random.normal(ks[10], (DEPTH, D_MODEL), f32),
        "w_up": jax.random.normal(ks[11], (DEPTH, D_MODEL, D_FF), f32) * D_MODEL ** -0.5,
        "w_down": jax.random.normal(ks[12], (DEPTH, D_FF, D_MODEL), f32) * D_FF ** -0.5,
        "norm_final_g": 1.0 + 0.02 * jax.random.normal(ks[13], (D_MODEL,), f32),
    }


def reference(x_prompt, x_sample, cache_kv_w128, cache_kv_w512, cache_kv_w2048, state_conv,
              norm_attn_g, w_in, conv_w, w_out, norm_mlp_g, w_up, w_down, norm_final_g):
    y_prompt, kv128_p, kv512_p, kv2048_p, conv_p = run_trunk(
        x_prompt, None, None, norm_attn_g, w_in, conv_w, w_out,
        norm_mlp_g, w_up, w_down, norm_final_g, True)
    y_sample, kv128_s, kv512_s, kv2048_s, conv_s = run_trunk(
        x_sample, [cache_kv_w128, cache_kv_w512, cache_kv_w2048], state_conv,
        norm_attn_g, w_in, conv_w, w_out, norm_mlp_g, w_up, w_down, norm_final_g, False)
    return (y_prompt, y_sample, kv128_p, kv512_p, kv2048_p, conv_p,
            kv128_s, kv512_s, kv2048_s, conv_s)
```

```python
import itertools
import numpy as np
import concourse.bass as bass
import concourse.mybir as mybir
from concourse.bass_utils import run_bass_kernel_spmd

F32 = mybir.dt.float32
BF16 = mybir.dt.bfloat16
AF = mybir.ActivationFunctionType
ALU = mybir.AluOpType

NCORES = 8
D = 1024
SEQ = 8192
NSB = 16
NTS = 128
NTOK = SEQ + NTS
EPS = 1e-6
DIL = (1, 4, 16)
VW = 72
WIN = (128, 512, 2048)
ENGS = ("pe", "act", "dve", "pool", "sp")
CFG = {"d2d": 1, "nst": 4, "sample": 1, "phaseB": 1, "ntb": 33}


class Sched:
    def __init__(self):
        self.ops = []
        self.res = {}
        self.bar = {}

    def op(self, eng, fn, reads=(), writes=(), dma=None):
        idx = len(self.ops)
        ops = self.ops
        dom = ("dma", dma) if dma is not None else ("eng", eng)
        need = {}

        def add(j, kind):
            od = ops[j]["dom"]
            if dma is None and od == dom:
                if eng == "pe" or kind != "raw":
                    return
            if dma is not None and od == dom and kind == "waw":
                return
            if need.get(od, -1) < j:
                need[od] = j

        for r in reads:
            st = self.res.get(r)
            if st is None:
                st = self.res[r] = [[], [], []]
            for j in st[0]:
                add(j, "raw")
        for w in writes:
            st = self.res.get(w)
            if st is None:
                st = self.res[w] = [[], [], []]
            if st[1]:
                for j in st[1]:
                    add(j, "war")
                for j in st[0]:
                    add(j, "waw")
            else:
                for j in st[2]:
                    add(j, "war")
        for r in reads:
            self.res[r][1].append(idx)
        for w in writes:
            st = self.res[w]
            if st[1]:
                st[0] = [idx]
                st[2] = st[1]
                st[1] = []
            else:
                st[0].append(idx)
        for od, j in self.bar.items():
            if dma is None and od == dom:
                continue
            if need.get(od, -1) < j:
                need[od] = j
        keep = list(need.values())
        ops.append(dict(eng=eng, fn=fn, dom=dom, deps=keep, sig=(dma is not None), cnt=None))
        for j in keep:
            ops[j]["sig"] = True
        return idx

    def barrier(self):
        last = {}
        for i, o in enumerate(self.ops):
            last[o["dom"]] = i
        self.bar = last
        for j in last.values():
            self.ops[j]["sig"] = True

    def domains(self):
        return sorted(set(o["dom"] for o in self.ops))

    def emit(self, sems, block):
        cnt = {}
        for o in self.ops:
            if o["sig"]:
                inc = 16 if o["dom"][0] == "dma" else 1
                cnt[o["dom"]] = cnt.get(o["dom"], 0) + inc
                o["cnt"] = cnt[o["dom"]]
        per_eng = {e: [] for e in ENGS}
        for o in self.ops:
            per_eng[o["eng"]].append(o)
        ops = self.ops

        def run(engname, e):
            seen = {}
            mydom = ("eng", engname)
            for o in per_eng[engname]:
                for j in o["deps"]:
                    d = ops[j]
                    dom, v = d["dom"], d["cnt"]
                    if seen.get(dom, 0) >= v:
                        continue
                    e.wait_ge(sems[dom], v)
                    seen[dom] = v
                ins = o["fn"](e)
                if o["sig"]:
                    ins.then_inc(sems[o["dom"]], 16 if o["dom"][0] == "dma" else 1)
            last = {}
            for o in per_eng[engname]:
                if o["dom"][0] == "dma":
                    last[o["dom"]] = o["cnt"]
            for dom, v in last.items():
                if seen.get(dom, 0) < v:
                    e.wait_ge(sems[dom], v)

        block.tensor(lambda e: run("pe", e))
        block.scalar(lambda e: run("act", e))
        block.vector(lambda e: run("dve", e))
        block.gpsimd(lambda e: run("pool", e))
        block.sync(lambda e: run("sp", e))


class Mem:
    def __init__(self, nc, base=16384, limit=16384 + 212800):
        self.nc, self.off, self.limit = nc, base, limit
        self.n = 0
        self.offs = {}

    def alloc(self, name, shape, dt, at=None):
        nb = int(np.prod(shape[1:])) * (4 if dt == F32 else 2)
        nb = (nb + 63) // 64 * 64
        off = self.off if at is None else at
        t = self.nc.alloc_sbuf_tensor_at("%s_%d" % (name, self.n), list(shape), dt, offset=off)
        self.offs[name] = off
        self.n += 1
        if at is None:
            self.off += nb
            assert self.off <= self.limit, (name, self.off)
        return t


def f_mm(out, lhsT, rhs, start, stop, skip=False):
    if skip:
        return lambda e: e.matmul(out, lhsT=lhsT, rhs=rhs, start=start, stop=stop, skip_group_check=True)
    return lambda e: e.matmul(out, lhsT=lhsT, rhs=rhs, start=start, stop=stop)


def f_tr(out, in_, ident):
    return lambda e: e.transpose(out=out, in_=in_, identity=ident)


def f_act(out, in_, func, scale=None, bias=None, accum_out=None):
    kw = {}
    if scale is not None:
        kw["scale"] = scale
    if bias is not None:
        kw["bias"] = bias
    if accum_out is not None:
        kw["accum_out"] = accum_out
    return lambda e: e.activation(out=out, in_=in_, func=func, **kw)


def f_copy(out, in_):
    return lambda e: e.tensor_copy(out=out, in_=in_)


def f_acopy(out, in_):
    return lambda e: e.activation(out=out, in_=in_, func=AF.Copy)


def f_tt(out, in0, in1, op):
    return lambda e: e.tensor_tensor(out=out, in0=in0, in1=in1, op=op)


def f_ts(out, in0, s1, op0, s2=None, op1=None):
    if op1 is None:
        return lambda e: e.tensor_scalar(out=out, in0=in0, scalar1=s1, scalar2=None, op0=op0)
    return lambda e: e.tensor_scalar(out=out, in0=in0, scalar1=s1, scalar2=s2, op0=op0, op1=op1)


def f_stt(out, in0, scalar, in1, op0, op1):
    return lambda e: e.scalar_tensor_tensor(out=out, in0=in0, scalar=scalar, in1=in1, op0=op0, op1=op1)


def f_recip(out, in_):
    return lambda e: e.reciprocal(out=out, in_=in_)


def f_memset(ap, v):
    return lambda e: e.memset(ap, v)


def f_dma(out, in_, slow=False):
    if slow:
        return lambda e: e.dma_start(out=out, in_=in_, allow_slow_non_contiguous=True)
    return lambda e: e.dma_start(out=out, in_=in_)


def f_asel(out, in_, pattern, cmp, fill, base, cm):
    return lambda e: e.affine_select(out=out, in_=in_, pattern=pattern, compare_op=cmp, fill=fill,
                                     base=base, channel_multiplier=cm)


def build_nc():
    nc = bass.Bass("TRN2", target_bir_lowering=False)
    dr = lambda n, s, k: nc.dram_tensor(n, list(s), F32, kind=k).ap()
    xp = dr("xp", [SEQ, D], "ExternalInput")
    xs = dr("xs", [NTS, D], "ExternalInput")
    ck = [dr("ck%d" % g, [NSB, WIN[g], 512], "ExternalInput") for g in range(3)]
    sconv = dr("sconv", [NSB, 2, 256], "ExternalInput")
    w_in = dr("w_in", [D, 3072], "ExternalInput")
    w_out = dr("w_out", [D, D], "ExternalInput")
    w_up = dr("w_up", [D, 4096], "ExternalInput")
    w_down = dr("w_down", [4096, D], "ExternalInput")
    g_attn = dr("g_attn", [D], "ExternalInput")
    g_mlp = dr("g_mlp", [D], "ExternalInput")
    g_fin = dr("g_fin", [D], "ExternalInput")
    conv_w = dr("conv_w", [3, 256], "ExternalInput")
    yp = dr("yp", [SEQ, D], "ExternalOutput")
    ys = dr("ys", [NTS, D], "ExternalOutput")
    kvp = [dr("kvp%d" % g, [WIN[g], 512], "ExternalOutput") for g in range(3)]
    convp = dr("convp", [2, 256], "ExternalOutput")
    kvs = [dr("kvs%d" % g, [NSB, WIN[g], 512], "ExternalOutput") for g in range(3)]
    convs = dr("convs", [NSB, 2, 256], "ExternalOutput")
    NZ = dr("NZ", [NTOK, 780], "Internal")
    cvT = nc.dram_tensor("cvT", [256, NTOK], BF16, kind="Internal").ap()

    S = Sched()
    mem = Mem(nc)

    pp = [nc.alloc_psum_tensor("pp%d" % i, [128, 1024], F32) for i in range(4)]

    def bank(k):
        return pp[k // 2][:, (k % 2) * 512:(k % 2 + 1) * 512]

    def bank_bf(k):
        return bank(k).bitcast(BF16)

    ident = mem.alloc("ident", [128, 128], BF16)
    mask2 = mem.alloc("mask2", [128, 2, 256], BF16)
    epsb = mem.alloc("epsb", [128, 1], F32)
    g_attn_sb = mem.alloc("g_attn_sb", [128, 8], F32)
    g_mlp_sb = mem.alloc("g_mlp_sb", [128, 8], F32)
    cw_sb = mem.alloc("cw_sb", [128, 2, 3], F32)
    tmpf = mem.alloc("tmpf", [128, 256], F32)
    common_end = mem.off

    S.op("pool", f_memset(epsb[:], EPS), writes=["epsb"])
    S.op("pool", f_memset(tmpf[:, 0:128], 0.0), writes=["tmpf"])
    S.op("pool", f_asel(tmpf[:, 0:128], tmpf[:, 0:128], [[-1, 128]], ALU.not_equal, 1.0, 0, 1),
         reads=["tmpf"], writes=["tmpf"])
    S.op("dve", f_copy(ident[:], tmpf[:, 0:128]), reads=["tmpf"], writes=["ident"])
    S.op("pool", f_memset(tmpf[:], 1.0), reads=["tmpf"], writes=["tmpf"])
    S.op("pool", f_asel(tmpf[:, 0:128], tmpf[:, 0:128], [[-1, 128]], ALU.is_ge, 0.0, 0, 1),
         reads=["tmpf"], writes=["tmpf"])
    S.op("pool", f_asel(tmpf[:, 128:256], tmpf[:, 128:256], [[1, 128]], ALU.is_ge, 0.0, 0, -1),
         reads=["tmpf"], writes=["tmpf"])
    S.op("dve", f_copy(mask2[:, 0, :], tmpf[:]), reads=["tmpf"], writes=["mask2"])
    S.op("dve", f_copy(mask2[:, 1, :], tmpf[:]), reads=["tmpf"], writes=["mask2"])
    S.op("sp", f_dma(g_attn_sb[:], g_attn.rearrange("(k p) -> p k", p=128), slow=True),
         writes=["g_attn_sb"], dma="ld_ga")
    S.op("sp", f_dma(g_mlp_sb[:], g_mlp.rearrange("(k p) -> p k", p=128), slow=True),
         writes=["g_mlp_sb"], dma="ld_gm")
    for cc_ in range(2):
        S.op("sp", f_dma(cw_sb[:, cc_, :], conv_w[:, cc_ * 128:(cc_ + 1) * 128].rearrange("j p -> p j"), slow=True),
             writes=["cw_sb"], dma="ld_cw")

    d2d_next = [0]

    def d2d_chunk():
        bb = d2d_next[0]
        if bb >= NSB or not CFG["d2d"]:
            return
        d2d_next[0] += 1
        for g in range(3):
            W = WIN[g]
            S.op("act", f_dma(kvs[g][bb, 0:W - 8, :], ck[g][bb, 8:W, :]), dma="d2d%d" % g)

    w_in_sb = mem.alloc("w_in_sb", [128, 8, 3072], BF16)
    xnT = mem.alloc("xnT", [128, 8, 2048], BF16)
    xt = [mem.alloc("xt%d" % i, [128, D], F32) for i in range(2)]
    xn = [mem.alloc("xn%d" % i, [128, D], BF16) for i in range(2)]
    ssb = [mem.alloc("ss%d" % i, [128, 1], F32) for i in range(2)]
    rstd = [mem.alloc("rstd%d" % i, [128, 1], F32) for i in range(2)]
    QT = mem.alloc("QT", [128, 2, 2048], BF16)
    KTc = mem.alloc("KTc", [128, 2, 2048], BF16)
    VT = mem.alloc("VT", [128, 2, 2048], BF16)
    Vc = mem.alloc("Vc", [128, 16, 4, VW], BF16)
    KTh = [mem.alloc("KTh%d" % g, [128, 2, DIL[g] * 128], BF16) for g in range(3)]
    Vh = [mem.alloc("Vh%d" % g, [128, DIL[g], 4, VW], BF16) for g in range(3)]
    NPT = 4
    PT = [mem.alloc("PT%d" % i, [128, 2, 256], BF16) for i in range(NPT)]
    osb = [mem.alloc("osb%d" % i, [128, 4, 65], F32) for i in range(2)]
    Bsb = [mem.alloc("Bsb%d" % i, [128, 512], F32) for i in range(2)]
    Csb = [mem.alloc("Csb%d" % i, [128, 512], F32) for i in range(2)]
    ubuf = [mem.alloc("ubuf%d" % i, [128, 514], F32) for i in range(2)]
    ybuf = [mem.alloc("ybuf%d" % i, [128, 512], F32) for i in range(2)]
    cvo = [mem.alloc("cvo%d" % i, [128, 512], BF16) for i in range(2)]
    kvst = [mem.alloc("kvst%d" % i, [128, 512], F32) for i in range(2)]
    sQT = [mem.alloc("sQT%d" % g, [128, 2, 128], BF16) for g in range(3)]
    sKT = [mem.alloc("sKT%d" % g, [128, 2, 128], BF16) for g in range(3)]
    sVT = [mem.alloc("sVT%d" % g, [128, 2, 128], BF16) for g in range(3)]
    sVn = [mem.alloc("sVn%d" % g, [128, 4, VW], BF16) for g in range(3)]
    NKB = 3
    Kb = [mem.alloc("Kb%d" % i, [128, 256], BF16) for i in range(NKB)]
    sKTc = [mem.alloc("sKTc%d" % i, [128, 2, 128], BF16) for i in range(NKB)]
    NVC = 5
    sVc = [mem.alloc("sVc%d" % i, [128, 4, VW], BF16) for i in range(NVC)]
    NTE = 3
    tmpE = [mem.alloc("tmpE%d" % i, [128, 4, 8], F32) for i in range(NTE)]
    NPF = 4
    PTf = [mem.alloc("PTf%d" % i, [128, 2, 2, 128], BF16) for i in range(NPF)]
    PTn = mem.alloc("PTn", [128, 2, 2, 128], BF16)
    smask = mem.alloc("smask", [128, 13, 8], F32)
    nmask = mem.alloc("nmask", [128, 3, 128], BF16)
    ubs = [mem.alloc("ubs%d" % i, [128, NSB, 10], F32) for i in range(2)]
    kvss = [mem.alloc("kvss%d" % g, [128, 512], F32) for g in range(3)]
    osbs = mem.alloc("osbs", [128, 3, 260], F32)
    assert mem.offs["osbs"] + 3072 - mem.offs["kvss0"] >= 8192
    xt.append(mem.alloc("xt2", [128, D], F32, at=mem.offs["kvss0"]))
    xt.append(mem.alloc("xt3", [128, D], F32, at=mem.offs["kvss0"] + 4096))
    NXT = 4
    xo = mem.offs["PT0"]
    assert mem.offs["kvst1"] + 2048 - xo >= 2048 + 8192 + 16384
    kvc = [mem.alloc("kvc0", [128, 1, 512], F32, at=xo),
           mem.alloc("kvc1", [128, 4, 512], F32, at=xo + 2048),
           mem.alloc("kvc2", [128, 8, 512], F32, at=xo + 2048 + 8192)]
    KVC_ALIAS = (["PT%d" % i for i in range(NPT)] + ["osb0", "osb1"] +
                 [n + str(i) for n in ("Bsb", "Csb", "ubuf", "ybuf", "cvo", "kvst") for i in range(2)])
    memB = Mem(nc, base=common_end)
    w_out_sb = memB.alloc("w_out_sb", [128, 8, 1024], BF16)
    w_up_sb = memB.alloc("w_up_sb", [128, 8, 4096], BF16)
    w_down_sb = memB.alloc("w_down_sb", [128, 32, 1024], BF16)
    assert memB.off <= mem.offs["PT0"], (memB.off, mem.offs["PT0"])
    WB_ALIAS = (["w_in"] + ["xnT%d" % i for i in range(4)] + [n + str(i) for n in ("xt", "xn", "ss", "rstd") for i in range(2)] + ["xt2", "xt3"] +
                ["QT", "KTc", "VT", "Vc"] + ["KTh%d" % g for g in range(3)] + ["Vh%d" % g for g in range(3)])

    def load_phaseB_weights():
        first = [True]

        def al():
            r = WB_ALIAS if first[0] else []
            first[0] = False
            return r
        w_out_v = w_out.rearrange("(k p) e -> p k e", p=128)
        for kc in range(8):
            S.op("pool", f_dma(w_out_sb[:, kc, :], w_out_v[:, kc, :]), writes=["w_out"] + al(), dma="ld_wo")
        w_up_v = w_up.rearrange("(k p) e -> p k e", p=128)
        for cb in range(8):
            S.op("pool", f_dma(w_up_sb[:, :, cb * 512:(cb + 1) * 512], w_up_v[:, :, cb * 512:(cb + 1) * 512]),
                 writes=["w_up%d" % cb], dma="ld_wu%d" % cb)
        w_down_v = w_down.rearrange("(c p) e -> p c e", p=128)
        for cb in range(8):
            S.op("pool", f_dma(w_down_sb[:, cb * 4:(cb + 1) * 4, :], w_down_v[:, cb * 4:(cb + 1) * 4, :]),
                 writes=["w_down%d" % cb], dma="ld_wd%d" % cb)

    w_in_v = w_in.rearrange("(k p) e -> p k e", p=128)
    for kc in range(8):
        S.op("pool", f_dma(w_in_sb[:, kc, :], w_in_v[:, kc, :]), writes=["w_in"], dma="ld_w_in")

    for cc_ in range(2):
        for j_ in range(2):
            S.op("sp", f_dma(ubs[cc_][:, :, j_], sconv[:, j_, cc_ * 128:(cc_ + 1) * 128].rearrange("b p -> p b"), slow=True),
                 writes=["ubs%d" % cc_], dma="ld_sc%d" % cc_)
    S.op("pool", f_memset(Vc[:, :, :, 64:VW], 1.0), writes=["Vc"])
    for g in range(3):
        S.op("pool", f_memset(Vh[g][:, :, :, 64:VW], 1.0), writes=["Vh%d" % g])
        S.op("pool", f_memset(sVn[g][:, :, 64:VW], 1.0), writes=["sVn%d" % g])
    for i in range(NVC):
        S.op("pool", f_memset(sVc[i][:, :, 64:VW], 1.0), writes=["sVc%d" % i])
    for i in range(2):
        S.op("pool", f_memset(ubuf[i][:, 0:2], 0.0), writes=["ubuf%d" % i])
    for i in range(NPF):
        S.op("pool", f_memset(PTf[i][:], 0.0), writes=["PTf%d" % i])
    S.op("pool", f_memset(smask[:], 0.0), writes=["smask"])
    S.op("pool", f_memset(smask[:, 0, :], 1.0), reads=["smask"], writes=["smask"])
    S.op("pool", f_asel(smask[:, 0, :], smask[:, 0, :], [[-1, 8]], ALU.is_ge, 0.0, 0, 1),
         reads=["smask"], writes=["smask"])
    for rho in range(4):
        S.op("pool", f_memset(smask[:, 1 + rho, rho:rho + 1], 1.0), reads=["smask"], writes=["smask"])
        S.op("pool", f_memset(smask[:, 1 + rho, rho + 4:rho + 5], 1.0), reads=["smask"], writes=["smask"])
        S.op("pool", f_memset(smask[0:1, 1 + rho, rho + 4:rho + 5], 0.0), reads=["smask"], writes=["smask"])
    for rho in range(8):
        S.op("pool", f_memset(smask[:, 5 + rho, rho:rho + 1], 1.0), reads=["smask"], writes=["smask"])
    blkpat = [[-8, 16], [0, 8]]
    t3 = lambda ap: ap.rearrange("p (b t) -> p b t", t=8)
    S.op("pool", f_memset(tmpf[:, 0:128], 1.0), reads=["tmpf"], writes=["tmpf"])
    S.op("pool", f_asel(tmpf[:, 0:128], tmpf[:, 0:128], [[1, 128]], ALU.is_ge, 0.0, 0, -1),
         reads=["tmpf"], writes=["tmpf"])
    S.op("pool", f_asel(t3(tmpf[:, 0:128]), t3(tmpf[:, 0:128]), blkpat, ALU.is_ge, 0.0, 0, 1),
         reads=["tmpf"], writes=["tmpf"])
    S.op("dve", f_copy(nmask[:, 0, :], tmpf[:, 0:128]), reads=["tmpf"], writes=["nmask"])
    S.op("pool", f_memset(tmpf[:, 0:128], 0.0), reads=["tmpf"], writes=["tmpf"])
    S.op("pool", f_asel(tmpf[:, 0:128], tmpf[:, 0:128], [[1, 128]], ALU.not_equal, 1.0, 0, -1),
         reads=["tmpf"], writes=["tmpf"])
    S.op("pool", f_asel(tmpf[:, 0:128], tmpf[:, 0:128], [[1, 128]], ALU.not_equal, 1.0, -4, -1),
         reads=["tmpf"], writes=["tmpf"])
    S.op("pool", f_asel(t3(tmpf[:, 0:128]), t3(tmpf[:, 0:128]), blkpat, ALU.is_ge, 0.0, 0, 1),
         reads=["tmpf"], writes=["tmpf"])
    S.op("dve", f_copy(nmask[:, 1, :], tmpf[:, 0:128]), reads=["tmpf"], writes=["nmask"])
    S.op("dve", f_copy(nmask[:, 2, :], ident[:]), reads=["ident"], writes=["nmask"])

    ROLE_P = (0, 1)
    ROLE_O = (6,)
    B_XT = 7
    B_VT = 7
    SPAIR = (pp[1], pp[2], pp[0])
    STOK = (["pairS0"], ["pairS1"], ["bank0", "bank1"])

    def spair(k):
        return SPAIR[k][:, :].rearrange("p (b x) -> p b x", b=2)
    cnt = {"x": 0, "p": 0, "ev": 0, "s": 0, "o": 0, "pt": 0, "os": 0, "kv": 0}
    bk = lambda b: "bank%d" % b

    def xtile_l(src_ap):
        i = cnt["x"]
        cnt["x"] += 1
        sl = i % NXT
        S.op("act", f_dma(xt[sl][:], src_ap), writes=["xt%d" % sl], dma="ldx%d" % sl)
        return sl

    def xtile_c(xs_):
        sl = xs_ % 2
        S.op("act", f_act(xn[sl][:], xt[xs_][:], AF.Square, accum_out=ssb[sl][:]),
             reads=["xt%d" % xs_], writes=["xn%d" % sl, "ss%d" % sl])
        S.op("act", f_act(rstd[sl][:], ssb[sl][:], AF.Ln, scale=1.0 / D, bias=epsb[:, 0:1]),
             reads=["ss%d" % sl, "epsb"], writes=["rstd%d" % sl])
        S.op("act", f_act(rstd[sl][:], rstd[sl][:], AF.Exp, scale=-0.5),
             reads=["rstd%d" % sl], writes=["rstd%d" % sl])
        S.op("dve", f_ts(xn[sl][:], xt[xs_][:], rstd[sl][:, 0:1], ALU.mult),
             reads=["xt%d" % xs_, "rstd%d" % sl, "xn%d" % sl], writes=["xn%d" % sl])

    def xtile_b(xs_, col0, xtok):
        sl = xs_ % 2
        psx = bank_bf(B_XT).rearrange("p (k t) -> p k t", k=8)
        for kc in range(8):
            S.op("pe", f_tr(psx[:, kc, :], xn[sl][:, kc * 128:(kc + 1) * 128], ident[:]),
                 reads=["xn%d" % sl, "ident"], writes=[bk(B_XT)])
        gb = g_attn_sb[:, :].unsqueeze(2).broadcast_to([128, 8, 128])
        S.op("dve", f_tt(xnT[:, :, col0:col0 + 128], psx, gb, ALU.mult),
             reads=[bk(B_XT), "g_attn_sb"], writes=[xtok])

    def xtile(src_ap, col0, xtok):
        sl = xtile_l(src_ap)
        xtile_c(sl)
        xtile_b(sl, col0, xtok)

    def xsteps(srcs):
        n = len(srcs)
        slot = {}

        def mk_l(k):
            def f():
                slot[k] = xtile_l(srcs[k][0])
            return f

        def mk_c(k):
            return lambda: xtile_c(slot[k])

        def mk_b(k):
            return lambda: xtile_b(slot[k], srcs[k][1], srcs[k][2])
        pre = [mk_l(k) for k in range(min(NXT, n))]
        groups = []
        for k in range(n):
            g_ = []
            if k + 1 < n:
                g_.append(mk_c(k + 1))
            g_.append(mk_b(k))
            if k + NXT < n:
                g_.append(mk_l(k + NXT))
            groups.append(g_)
        return pre, mk_c(0), groups

    def run_all_steps(steps):
        for f in steps:
            f()

    def run_group(grp):
        if grp is not None:
            for f in grp:
                f()

    def run_with(gen, groups):
        for _ in gen:
            run_group(next(groups, None))
        for grp in groups:
            run_group(grp)

    def evac(out_ap, in_ap, reads, writes):
        k = cnt["ev"]
        cnt["ev"] += 1
        if k % 2 == 0:
            S.op("act", f_acopy(out_ap, in_ap), reads=reads, writes=writes)
        else:
            S.op("dve", f_copy(out_ap, in_ap), reads=reads, writes=writes)

    def proj_chunk(col, ntok, tok0, xtoks):
        b = ROLE_P[cnt["p"] % 2]
        cnt["p"] += 1
        for kc in range(8):
            S.op("pe", f_mm(bank(b)[:, 0:ntok], w_in_sb[:, kc, col:col + 128], xnT[:, kc, tok0:tok0 + ntok],
                            kc == 0, kc == 7), reads=["w_in"] + xtoks, writes=[bk(b)])
        return b

    def tokmajor_kv(g, tok0, xtoks):
        b = ROLE_P[cnt["p"] % 2]
        cnt["p"] += 1
        for t in range(2):
            col = (1 + t) * 768 + g * 256
            for kc in range(8):
                S.op("pe", f_mm(bank(b)[:, t * 256:(t + 1) * 256], xnT[:, kc, tok0:tok0 + 128],
                                w_in_sb[:, kc, col:col + 256], kc == 0, kc == 7),
                     reads=["w_in"] + xtoks, writes=[bk(b)])
        return b

    def group_proj(g, ct, steps=None):
        d = DIL[g]
        m = 512 // d
        for t, dest, tok in ((0, QT, "QT"), (1, KTc, "KTc"), (2, VT, "VT")):
            for p in range(2):
                col = t * 768 + g * 256 + p * 128
                b = proj_chunk(col, 512, ct * 512, ["xnT%d" % ct])
                out_ap = dest[:, p, :].rearrange("p (r m) -> p r m", r=d)[:, :, ct * m:(ct + 1) * m]
                in_ap = bank(b).rearrange("p (m r) -> p r m", r=d)
                evac(out_ap, in_ap, [bk(b)], [tok])
                if steps is not None:
                    run_group(next(steps, None))

    def v_transposes(g):
        for r in range(8):
            b_ = (B_VT, ROLE_O[0])[r % 2]
            psv = bank_bf(b_)
            for c2 in range(2):
                c = 2 * r + c2
                for p in range(2):
                    S.op("pe", f_tr(psv[:, c2 * 256 + p * 128:c2 * 256 + (p + 1) * 128], VT[:, p, c * 128:(c + 1) * 128],
                                    ident[:]), reads=["VT", "ident"], writes=[bk(b_)])
            S.op("dve", f_copy(Vc[:, 2 * r:2 * r + 2, :, 0:64],
                               psv[:, 0:512].rearrange("p (c h e) -> p c h e", c=2, h=4)),
                 reads=[bk(b_)], writes=["Vc"])

    def attention(g, st, filler=None, every=3):
        d = DIL[g]
        nblk = 16 // d
        units = []
        for rho in range(d):
            for j in range(nblk):
                for p in range(2):
                    units.append((rho, j, p))
        LAG = 3
        state = {}

        def emit_S(u):
            rho, j, p = units[u]
            c = rho * nblk + j
            has_prev = not (st == 0 and j == 0)
            sb_ = cnt["s"] % 3
            cnt["s"] += 1
            ps = spair(sb_)
            stok = STOK[sb_]
            if has_prev:
                for hh in range(2):
                    rows = slice(hh * 64, hh * 64 + 64)
                    if j == 0:
                        kprev = KTh[g][rows, p, rho * 128:(rho + 1) * 128]
                        ktok = "KTh%d" % g
                    else:
                        kprev = KTc[rows, p, (c - 1) * 128:c * 128]
                        ktok = "KTc"
                    S.op("pe", f_mm(ps[:, hh, 0:128], kprev, QT[rows, p, c * 128:(c + 1) * 128], True, True),
                         reads=["QT", ktok], writes=stok)
            for hh in range(2):
                rows = slice(hh * 64, hh * 64 + 64)
                S.op("pe", f_mm(ps[:, hh, 128:256], KTc[rows, p, c * 128:(c + 1) * 128],
                                QT[rows, p, c * 128:(c + 1) * 128], True, True),
                     reads=["QT", "KTc"], writes=stok)
            pt = cnt["pt"] % NPT
            cnt["pt"] += 1
            lo = 0 if has_prev else 128
            if CFG.get("att", 9) >= 2:
                S.op("act", f_act(PT[pt][:, :, lo:256], ps[:, :, lo:256], AF.Exp, scale=0.125),
                     reads=stok, writes=["PT%d" % pt])
                S.op("dve", f_tt(PT[pt][:, :, lo:256], PT[pt][:, :, lo:256], mask2[:, :, lo:256], ALU.mult),
                     reads=["PT%d" % pt, "mask2"], writes=["PT%d" % pt])
            state[u] = (pt, has_prev)

        def emit_PV(u):
            rho, j, p = units[u]
            c = rho * nblk + j
            pt, has_prev = state.pop(u)
            if CFG.get("att", 9) < 4:
                return
            if p == 0:
                state["o"] = ROLE_O[cnt["o"] % len(ROLE_O)]
                cnt["o"] += 1
            ob = state["o"]
            po = bank(ob)[:, 0:260].rearrange("p (h e) -> p h e", h=4)
            for hh in range(2):
                h = 2 * p + hh
                if has_prev:
                    if j == 0:
                        vprev, vtok = Vh[g][:, rho, h, 0:65], "Vh%d" % g
                    else:
                        vprev, vtok = Vc[:, c - 1, h, 0:65], "Vc"
                    S.op("pe", f_mm(po[:, h, :], PT[pt][:, hh, 0:128], vprev, True, False),
                         reads=["PT%d" % pt, vtok], writes=[bk(ob)])
                S.op("pe", f_mm(po[:, h, :], PT[pt][:, hh, 128:256], Vc[:, c, h, 0:65], not has_prev, True),
                     reads=["PT%d" % pt, "Vc"], writes=[bk(ob)])
            if p == 1 and CFG.get("att", 9) >= 5:
                os_ = cnt["os"] % 2
                cnt["os"] += 1
                S.op("dve", f_copy(osb[os_][:], po), reads=[bk(ob)], writes=["osb%d" % os_])
                n0 = (st * 2048) // d + 128 * j
                dst = NZ.rearrange("(n r) c -> r n c", r=d)[rho, n0:n0 + 128, g * 260:(g + 1) * 260]
                S.op("sp", f_dma(dst, osb[os_][:].rearrange("p h e -> p (h e)")),
                     reads=["osb%d" % os_], writes=["NZ_st%d" % st], dma="stnz%d" % os_)

        n = len(units)
        for u in range(n + LAG):
            if u < n:
                emit_S(u)
            if u - LAG >= 0:
                emit_PV(u - LAG)
            if filler is not None and u % every == every - 1:
                next(filler, None)
        if filler is not None:
            for _ in filler:
                pass
        if st < 3:
            S.op("dve", f_copy(KTh[g][:].rearrange("p q (r m) -> p q r m", r=d),
                                KTc[:].rearrange("p q (r m) -> p q r m", r=d)[:, :, :, (nblk - 1) * 128:nblk * 128]),
                 reads=["KTc"], writes=["KTh%d" % g])
            S.op("act", f_acopy(Vh[g][:].rearrange("p r h e -> p r (h e)"),
                                Vc[:].rearrange("p (r j) h e -> p r j (h e)", r=d)[:, :, nblk - 1, :]),
                 reads=["Vc"], writes=["Vh%d" % g])

    def conv_pass(st, ct):
        tok0 = st * 2048 + ct * 512
        xtk = ["xnT%d" % ct]
        for cc in range(2):
            bB = proj_chunk(2304 + cc * 128, 512, ct * 512, xtk)
            S.op("act", f_acopy(Bsb[cc][:], bank(bB)), reads=[bk(bB)], writes=["Bsb%d" % cc])
            yield
            bC = proj_chunk(2560 + cc * 128, 512, ct * 512, xtk)
            S.op("act", f_acopy(Csb[cc][:], bank(bC)), reads=[bk(bC)], writes=["Csb%d" % cc])
            yield
            bH = proj_chunk(2816 + cc * 128, 512, ct * 512, xtk)
            u = ubuf[cc]
            S.op("dve", f_tt(u[:, 2:514], bank(bH), Csb[cc][:], ALU.mult),
                 reads=[bk(bH), "Csb%d" % cc], writes=["ubuf%d" % cc])
            y = ybuf[cc]
            S.op("dve", f_ts(y[:], u[:, 0:512], cw_sb[:, cc, 0:1], ALU.mult),
                 reads=["ubuf%d" % cc, "cw_sb"], writes=["ybuf%d" % cc])
            S.op("dve", f_stt(y[:], u[:, 1:513], cw_sb[:, cc, 1:2], y[:], ALU.mult, ALU.add),
                 reads=["ubuf%d" % cc, "cw_sb", "ybuf%d" % cc], writes=["ybuf%d" % cc])
            S.op("dve", f_stt(y[:], u[:, 2:514], cw_sb[:, cc, 2:3], y[:], ALU.mult, ALU.add),
                 reads=["ubuf%d" % cc, "cw_sb", "ybuf%d" % cc], writes=["ybuf%d" % cc])
            S.op("dve", f_tt(cvo[cc][:], Bsb[cc][:], y[:], ALU.mult),
                 reads=["Bsb%d" % cc, "ybuf%d" % cc], writes=["cvo%d" % cc])
            S.op("sp", f_dma(cvT[cc * 128:(cc + 1) * 128, tok0:tok0 + 512], cvo[cc][:]),
                 reads=["cvo%d" % cc], writes=["cvT_st%d" % st], dma="stcv%d" % cc)
            if st == 3 and ct == 3:
                S.op("sp", f_dma(convp.rearrange("j c -> c j")[cc * 128:(cc + 1) * 128, :], u[:, 512:514], slow=True),
                     reads=["ubuf%d" % cc], dma="stcp")
            S.op("dve", f_copy(u[:, 0:2], u[:, 512:514]), reads=["ubuf%d" % cc], writes=["ubuf%d" % cc])
            yield

    def conv_gen(st, cts):
        for ct in cts:
            yield from conv_pass(st, ct)

    def xt_gen(st):
        if st < 3:
            for ct in range(4):
                for s4 in range(4):
                    sub = ct * 4 + s4
                    t0 = (st + 1) * 2048 + sub * 128
                    xtile(xp[t0:t0 + 128, :], sub * 128, "xnT%d" % ct)
                    yield
        else:
            xtile(xs[:, :], 0, "xnT0")
            yield

    def kv_tail():
        for sub in range(16):
            for g in range(3):
                first_sub = 16 - WIN[g] // 128
                if sub < first_sub:
                    continue
                b = tokmajor_kv(g, sub * 128, ["xnT%d" % (sub // 4)])
                sl = cnt["kv"] % 2
                cnt["kv"] += 1
                evac(kvst[sl][:], bank(b), [bk(b)], ["kvst%d" % sl])
                r0 = (sub - first_sub) * 128
                S.op("sp", f_dma(kvp[g][r0:r0 + 128, :], kvst[sl][:]), reads=["kvst%d" % sl], dma="stkv%d" % sl)

    def xtiles_prompt(st, ct):
        for s4 in range(4):
            sub = ct * 4 + s4
            t0 = st * 2048 + sub * 128
            xtile(xp[t0:t0 + 128, :], sub * 128, "xnT%d" % ct)

    srcs0 = [(xp[sub * 128:(sub + 1) * 128, :], sub * 128, "xnT%d" % (sub // 4)) for sub in range(16)]
    pre0, c00, groups0 = xsteps(srcs0)
    run_all_steps(iter(pre0))
    c00()
    for grp in groups0[0:4]:
        run_group(grp)
    pending0 = iter(groups0[4:16])
    def run_all(gen):
        for _ in gen:
            pass

    pending = pending0
    for st in range(CFG["nst"]):
        if st < 3:
            srcs = []
            for sub in range(16):
                t0 = (st + 1) * 2048 + sub * 128
                srcs.append((xp[t0:t0 + 128, :], sub * 128, "xnT%d" % (sub // 4)))
        else:
            srcs = [(xs[:, :], 0, "xnT0")]
        pre, c0, groups = xsteps(srcs)
        for g in range(3):
            for ct in range(4):
                group_proj(g, ct, itertools.islice(pending, 4) if g == 0 else None)
            if g == 0:
                for grp in pending:
                    run_group(grp)
            v_transposes(g)
            if g == 2:
                run_all_steps(iter(pre))
            attention(g, st)
            d2d_chunk()
        if st == 3:
            kv_tail()
        cg = conv_pass(st, 0)
        next(cg)
        c0()
        for _ in cg:
            pass
        if st < 3:
            run_with(conv_pass(st, 1), iter(groups[0:4]))
            run_with(conv_pass(st, 2), iter(groups[4:8]))
            run_with(conv_pass(st, 3), iter(groups[8:12]))
            pending = iter(groups[12:16])
        else:
            run_with(conv_pass(st, 1), iter(groups))
            for _ in conv_pass(st, 2):
                pass
            for _ in conv_pass(st, 3):
                pass
            pending = iter(())
        d2d_chunk()
    while d2d_next[0] < NSB and CFG["d2d"]:
        d2d_chunk()

    def sample_phase():
        XS = ["xnT0"]
        for g in range(3):
            for t, dest, tok in ((0, sQT, "sQT"), (1, sKT, "sKT"), (2, sVT, "sVT")):
                for p in range(2):
                    col = t * 768 + g * 256 + p * 128
                    b = proj_chunk(col, 128, 0, XS)
                    evac(dest[g][:, p, :], bank(b)[:, 0:128], [bk(b)], ["%s%d" % (tok, g)])
        for g in range(3):
            b = tokmajor_kv(g, 0, XS)
            evac(kvss[g][:], bank(b), [bk(b)], ["kvss%d" % g, "xt2", "xt3"])
            W = WIN[g]
            for bb in range(NSB):
                S.op("sp", f_dma(kvs[g][bb, W - 8:W, :], kvss[g][bb * 8:(bb + 1) * 8, :]),
                     reads=["kvss%d" % g], dma="stkvs")
        late = []
        for cc in range(2):
            bB = proj_chunk(2304 + cc * 128, 128, 0, XS)
            S.op("act", f_acopy(Bsb[cc][:, 0:128], bank(bB)[:, 0:128]), reads=[bk(bB)], writes=["Bsb%d" % cc])
            bC = proj_chunk(2560 + cc * 128, 128, 0, XS)
            S.op("act", f_acopy(Csb[cc][:, 0:128], bank(bC)[:, 0:128]), reads=[bk(bC)], writes=["Csb%d" % cc])
            bH = proj_chunk(2816 + cc * 128, 128, 0, XS)
            u = ubs[cc]
            S.op("dve", f_tt(u[:, :, 2:10], t3(bank(bH)[:, 0:128]), t3(Csb[cc][:, 0:128]), ALU.mult),
                 reads=[bk(bH), "Csb%d" % cc], writes=["ubs%d" % cc])
            y = t3(ybuf[cc][:, 0:128])
            S.op("pool", f_ts(y, u[:, :, 0:8], cw_sb[:, cc, 0:1], ALU.mult),
                 reads=["ubs%d" % cc, "cw_sb"], writes=["ybuf%d" % cc])
            S.op("dve", f_stt(y, u[:, :, 1:9], cw_sb[:, cc, 1:2], y, ALU.mult, ALU.add),
                 reads=["ubs%d" % cc, "cw_sb", "ybuf%d" % cc], writes=["ybuf%d" % cc])
            S.op("dve", f_stt(y, u[:, :, 2:10], cw_sb[:, cc, 2:3], y, ALU.mult, ALU.add),
                 reads=["ubs%d" % cc, "cw_sb", "ybuf%d" % cc], writes=["ybuf%d" % cc])
            S.op("dve", f_tt(cvo[cc][:, 0:128], Bsb[cc][:, 0:128], ybuf[cc][:, 0:128], ALU.mult),
                 reads=["Bsb%d" % cc, "ybuf%d" % cc], writes=["cvo%d" % cc])
            S.op("sp", f_dma(cvT[cc * 128:(cc + 1) * 128, SEQ:SEQ + 128], cvo[cc][:, 0:128]),
                 reads=["cvo%d" % cc], writes=["cvT_s"], dma="stcv%d" % cc)
            for j_ in range(2):
                late.append((convs[:, j_, cc * 128:(cc + 1) * 128].rearrange("b p -> p b"), u[:, :, 8 + j_], "ubs%d" % cc))

        if CFG["phaseB"]:
            load_phaseB_weights()
        ACC = (0, 1, 6)
        psA = [bank(ACC[g])[:, 0:260].rearrange("p (h e) -> p h e", h=4) for g in range(3)]
        for g in range(3):
            S.op("dve", f_memset(bank(ACC[g])[:, 0:260], 0.0), writes=[bk(ACC[g])])
        for g in range(3):
            psv = bank_bf(B_VT)
            for p in range(2):
                S.op("pe", f_tr(psv[:, p * 128:(p + 1) * 128], sVT[g][:, p, :], ident[:]),
                     reads=["sVT%d" % g, "ident"], writes=[bk(B_VT)])
            S.op("dve", f_copy(sVn[g][:, :, 0:64], psv[:, 0:256].rearrange("p (h e) -> p h e", h=4)),
                 reads=[bk(B_VT)], writes=["sVn%d" % g])
            sb_ = cnt["s"] % 2
            cnt["s"] += 1
            ps = spair(sb_)
            stok = "pairS%d" % sb_
            for h in range(4):
                p, hh = h // 2, h % 2
                rows = slice(hh * 64, hh * 64 + 64)
                S.op("pe", f_mm(ps[:, hh, p * 128:(p + 1) * 128], sKT[g][rows, p, :], sQT[g][rows, p, :], True, True),
                     reads=["sKT%d" % g, "sQT%d" % g], writes=[stok])
            S.op("act", f_act(PTn[:].rearrange("k a b q -> k a (b q)"), ps[:, :, 0:256], AF.Exp, scale=0.125),
                 reads=[stok], writes=["PTn"])
            S.op("dve", f_tt(PTn[:].rearrange("k a b q -> k (a b) q"), PTn[:].rearrange("k a b q -> k (a b) q"),
                              nmask[:, g, :].unsqueeze(1).broadcast_to([128, 4, 128]), ALU.mult),
                 reads=["PTn", "nmask"], writes=["PTn"])
            for h in range(4):
                p, hh = h // 2, h % 2
                S.op("pe", f_mm(psA[g][:, h, :], PTn[:, hh, p, :], sVn[g][:, h, 0:65], False, False, skip=True),
                     reads=["PTn", "sVn%d" % g, bk(ACC[g])], writes=[bk(ACC[g])])
        ncls = (1, 4, 8)
        moff = (0, 1, 5)
        units = [(bb, g, cl) for bb in range(NSB) for g in range(3) for cl in range(ncls[g])]
        pairs = [(bb, g) for bb in range(NSB) for g in range(3)]

        def load_pair(i):
            if i >= len(pairs):
                return
            bb, g = pairs[i]
            src = ck[g][bb].rearrange("(m r) c -> m r c", r=DIL[g])[:, 0:ncls[g], :]
            wr = ["kvc%d" % g] + (KVC_ALIAS if i < 3 else [])
            S.op("sp", f_dma(kvc[g][:], src), writes=wr, dma="ldkv%d" % g)

        load_pair(0)
        load_pair(1)

        def stA(u):
            bb, g, cl = units[u]
            if cl == 0:
                load_pair(bb * 3 + g + 2)
            ki, vi = u % NKB, u % NVC
            S.op("act", f_acopy(Kb[ki][:], kvc[g][:, cl, 0:256]), reads=["kvc%d" % g], writes=["Kb%d" % ki])
            S.op("dve", f_copy(sVc[vi][:, :, 0:64], kvc[g][:, cl, 256:512].rearrange("p (h e) -> p h e", h=4)),
                 reads=["kvc%d" % g], writes=["sVc%d" % vi])

        def stT(u):
            ki = u % NKB
            psv = bank_bf(B_VT)
            for p in range(2):
                S.op("pe", f_tr(psv[:, p * 128:(p + 1) * 128], Kb[ki][:, p * 128:(p + 1) * 128], ident[:]),
                     reads=["Kb%d" % ki, "ident"], writes=[bk(B_VT)])
            S.op("act", f_acopy(sKTc[ki][:].rearrange("p a k -> p (a k)"), psv[:, 0:256]),
                 reads=[bk(B_VT)], writes=["sKTc%d" % ki])

        def stS(u):
            bb, g, cl = units[u]
            ki, ei, fi = u % NKB, u % NTE, u % NPF
            sb_ = u % 2
            ps = spair(sb_)
            stok = "pairS%d" % sb_
            for h in range(4):
                p, hh = h // 2, h % 2
                rows = slice(hh * 64, hh * 64 + 64)
                S.op("pe", f_mm(ps[:, hh, p * 8:(p + 1) * 8], sKTc[ki][rows, p, :],
                                sQT[g][rows, p, bb * 8:(bb + 1) * 8], True, True),
                     reads=["sKTc%d" % ki, "sQT%d" % g], writes=[stok])
            S.op("act", f_act(tmpE[ei][:].rearrange("k (a b) t -> k a (b t)", a=2), ps[:, :, 0:16], AF.Exp, scale=0.125),
                 reads=[stok], writes=["tmpE%d" % ei])
            mk = smask[:, moff[g] + cl, :].unsqueeze(1).broadcast_to([128, 4, 8])
            S.op("dve", f_tt(PTf[fi][:].rearrange("k a b q -> k (a b) q")[:, :, bb * 8:(bb + 1) * 8], tmpE[ei][:], mk, ALU.mult),
                 reads=["tmpE%d" % ei, "smask", "PTf%d" % fi], writes=["PTf%d" % fi])

        def stP(u):
            bb, g, cl = units[u]
            vi, fi = u % NVC, u % NPF
            for h in range(4):
                p, hh = h // 2, h % 2
                S.op("pe", f_mm(psA[g][:, h, :], PTf[fi][:, hh, p, :], sVc[vi][:, h, 0:65], False, False, skip=True),
                     reads=["PTf%d" % fi, "sVc%d" % vi, bk(ACC[g])], writes=[bk(ACC[g])])
            S.op("dve", f_memset(PTf[fi][:].rearrange("k a b q -> k (a b) q")[:, :, bb * 8:(bb + 1) * 8], 0.0),
                 reads=["PTf%d" % fi], writes=["PTf%d" % fi])

        nu = len(units)
        for i in range(nu + 3):
            if i < nu:
                stA(i)
            if 0 <= i - 1 < nu:
                stT(i - 1)
            if 0 <= i - 2 < nu:
                stS(i - 2)
            if 0 <= i - 3 < nu:
                stP(i - 3)
        for dst_, src_, tok_ in late:
            S.op("sp", f_dma(dst_, src_, slow=True), reads=[tok_], dma="stcs")
        for g in range(3):
            S.op("dve", f_copy(osbs[:, g, :], bank(ACC[g])[:, 0:260]), reads=[bk(ACC[g])], writes=["osbs", "xt3"])
        S.op("sp", f_dma(NZ[SEQ:SEQ + 128, :], osbs[:].rearrange("p g e -> p (g e)")), reads=["osbs"],
             writes=["NZ_s"], dma="stnz0")

    if CFG["sample"]:
        sample_phase()

    def phase_b():
        S.barrier()
        xtB = [memB.alloc("xtB%d" % i, [128, 2, D], F32) for i in range(2)]
        NZt = [memB.alloc("NZt%d" % i, [128, 2, 780], F32) for i in range(2)]
        attn_n = [memB.alloc("attn_n%d" % i, [128, 768], BF16) for i in range(2)]
        attnT = [memB.alloc("attnT%d" % i, [128, 8, 128], BF16) for i in range(2)]
        xn1 = [memB.alloc("xn1_%d" % i, [128, D], BF16) for i in range(2)]
        xn1T = [memB.alloc("xn1T%d" % i, [128, 8, 256], BF16) for i in range(2)]
        hr = [memB.alloc("hr%d" % i, [128, 256], BF16) for i in range(2)]
        NH = 8
        h1T = [memB.alloc("h1T%d" % i, [128, 256], BF16) for i in range(NH)]
        gfb = memB.alloc("gfb", [128, D], F32)
        junkB = memB.alloc("junkB", [128, D], BF16)
        Zt = [memB.alloc("Zt%d" % i, [128, 4], F32) for i in range(2)]
        Zr = [memB.alloc("Zr%d" % i, [128, 4], F32) for i in range(2)]
        ssB = [memB.alloc("ssB%d" % i, [128, 1], F32) for i in range(4)]
        rsB = [memB.alloc("rsB%d" % i, [128, 1], F32) for i in range(4)]

        S.op("sp", f_dma(gfb[:], g_fin.partition_broadcast(128)), writes=["gfb"], dma="ld_gf")
        if not CFG["sample"]:
            load_phaseB_weights()

        psD = [pp[1], pp[2]]
        B_U = (0, 1)
        B_M = 6
        B_T = 7
        bc = {"a": 0, "r": 0}
        NT_B = CFG["ntb"]

        def tile_info(T):
            if T < 32:
                return 2, T * 256, xp[T * 256:(T + 1) * 256, :], yp[T * 256:(T + 1) * 256, :], ["NZ_st%d" % (T // 8)], ["cvT_st%d" % (T // 8)]
            return 1, SEQ, xs[:, :], ys[:, :], ["NZ_s"], ["cvT_s"]

        def prep(T):
            nsub, tok0, xsrc, ydst, nztok, cvtok = tile_info(T)
            sl = T % 2
            S.op("sp", f_dma(xtB[sl][:, 0:nsub, :], xsrc.rearrange("(s p) e -> p s e", p=128)),
                 writes=["xtB%d" % sl], dma="ldxB%d" % sl)
            S.op("sp", f_dma(NZt[sl][:, 0:nsub, :], NZ[tok0:tok0 + nsub * 128, :].rearrange("(s p) c -> p s c", p=128)),
                 reads=nztok, writes=["NZt%d" % sl], dma="ldnz%d" % sl)
            for s in range(nsub):
                a = bc["a"] % 2
                bc["a"] += 1
                S.op("sp", f_dma(attnT[a][:, 6:8, :],
                                 cvT[:, tok0 + s * 128:tok0 + (s + 1) * 128].rearrange("(c p) t -> p c t", p=128)),
                     reads=cvtok, writes=["attnT%d" % a], dma="ldcv%d" % a)
                nzv = NZt[sl][:, s, :].rearrange("p (g h e) -> p g h e", g=3, h=4)
                S.op("dve", f_tt(Zt[a][:], nzv[:, 0, :, 64], nzv[:, 1, :, 64], ALU.add),
                     reads=["NZt%d" % sl], writes=["Zt%d" % a])
                S.op("dve", f_tt(Zt[a][:], Zt[a][:], nzv[:, 2, :, 64], ALU.add),
                     reads=["NZt%d" % sl, "Zt%d" % a], writes=["Zt%d" % a])
                S.op("dve", f_recip(Zr[a][:], Zt[a][:]), reads=["Zt%d" % a], writes=["Zr%d" % a])
                zb = Zr[a][:, :].unsqueeze(1).unsqueeze(3).broadcast_to([128, 3, 4, 64])
                S.op("dve", f_tt(attn_n[a][:].rearrange("p (g h e) -> p g h e", g=3, h=4), nzv[:, :, :, 0:64], zb, ALU.mult),
                     reads=["NZt%d" % sl, "Zr%d" % a], writes=["attn_n%d" % a])
                yield
                pst = bank_bf(B_T).rearrange("p (k t) -> p k t", k=8)
                for kc in range(6):
                    S.op("pe", f_tr(pst[:, kc, :], attn_n[a][:, kc * 128:(kc + 1) * 128], ident[:]),
                         reads=["attn_n%d" % a, "ident"], writes=[bk(B_T)])
                S.op("act", f_acopy(attnT[a][:, 0:6, :], pst[:, 0:6, :]), reads=[bk(B_T)], writes=["attnT%d" % a])
                yield
                xrow = xtB[sl][:, s, :]
                for half in range(2):
                    for kc in range(8):
                        S.op("pe", f_mm(bank(B_M), attnT[a][:, kc, :], w_out_sb[:, kc, half * 512:(half + 1) * 512],
                                        kc == 0, kc == 7), reads=["attnT%d" % a, "w_out"], writes=[bk(B_M)])
                    S.op("dve", f_tt(xrow[:, half * 512:(half + 1) * 512], xrow[:, half * 512:(half + 1) * 512],
                                     bank(B_M), ALU.add), reads=[bk(B_M), "xtB%d" % sl], writes=["xtB%d" % sl])
                r = bc["r"] % 4
                bc["r"] += 1
                S.op("act", f_act(xn1[a][:], xrow, AF.Square, accum_out=ssB[r][:]),
                     reads=["xtB%d" % sl], writes=["xn1_%d" % a, "ssB%d" % r])
                S.op("act", f_act(rsB[r][:], ssB[r][:], AF.Ln, scale=1.0 / D, bias=epsb[:, 0:1]),
                     reads=["ssB%d" % r, "epsb"], writes=["rsB%d" % r])
                S.op("act", f_act(rsB[r][:], rsB[r][:], AF.Exp, scale=-0.5), reads=["rsB%d" % r], writes=["rsB%d" % r])
                S.op("dve", f_ts(xn1[a][:], xrow, rsB[r][:, 0:1], ALU.mult),
                     reads=["xtB%d" % sl, "rsB%d" % r, "xn1_%d" % a], writes=["xn1_%d" % a])
                yield
                for kc in range(8):
                    S.op("pe", f_tr(pst[:, kc, :], xn1[a][:, kc * 128:(kc + 1) * 128], ident[:]),
                         reads=["xn1_%d" % a, "ident"], writes=[bk(B_T)])
                gb = g_mlp_sb[:, :].unsqueeze(2).broadcast_to([128, 8, 128])
                S.op("dve", f_tt(xn1T[sl][:, :, s * 128:(s + 1) * 128], pst, gb, ALU.mult),
                     reads=[bk(B_T), "g_mlp_sb"], writes=["xn1T%d" % sl])
                yield

        def mlp(T, filler):
            nsub, tok0, xsrc, ydst, nztok, cvtok = tile_info(T)
            sl = T % 2
            N = nsub * 128
            LAGU = 4

            def up(c):
                ub = B_U[c % 2]
                for kc in range(8):
                    S.op("pe", f_mm(bank(ub)[:, 0:N], w_up_sb[:, kc, c * 128:(c + 1) * 128], xn1T[sl][:, kc, 0:N],
                                    kc == 0, kc == 7), reads=["w_up%d" % (c // 4), "xn1T%d" % sl], writes=[bk(ub)])
                S.op("act", f_act(hr[c % 2][:, 0:N], bank(ub)[:, 0:N], AF.Relu), reads=[bk(ub)], writes=["hr%d" % (c % 2)])
                S.op("dve", f_tt(h1T[c % NH][:, 0:N], hr[c % 2][:, 0:N], hr[c % 2][:, 0:N], ALU.mult),
                     reads=["hr%d" % (c % 2)], writes=["h1T%d" % (c % NH)])

            def down(c):
                for s in range(nsub):
                    for half in range(2):
                        S.op("pe", f_mm(psD[s][:, half * 512:(half + 1) * 512], h1T[c % NH][:, s * 128:(s + 1) * 128],
                                        w_down_sb[:, c, half * 512:(half + 1) * 512], c == 0, c == 31),
                             reads=["h1T%d" % (c % NH), "w_down%d" % (c // 4)], writes=["psD%d" % s])

            for c in range(32 + LAGU):
                if c < 32:
                    up(c)
                if c - LAGU >= 0:
                    down(c - LAGU)
                if c % 2 == 1:
                    next(filler, None)
            for _ in filler:
                pass
            for s in range(nsub):
                xrow = xtB[sl][:, s, :]
                for half in range(2):
                    S.op("dve", f_tt(xrow[:, half * 512:(half + 1) * 512], xrow[:, half * 512:(half + 1) * 512],
                                     psD[s][:, half * 512:(half + 1) * 512], ALU.add),
                         reads=["psD%d" % s, "xtB%d" % sl], writes=["xtB%d" % sl])

            def epi():
                for s in range(nsub):
                    xrow = xtB[sl][:, s, :]
                    r = bc["r"] % 4
                    bc["r"] += 1
                    S.op("act", f_act(junkB[:], xrow, AF.Square, accum_out=ssB[r][:]),
                         reads=["xtB%d" % sl], writes=["junkB", "ssB%d" % r])
                    S.op("act", f_act(rsB[r][:], ssB[r][:], AF.Ln, scale=1.0 / D, bias=epsb[:, 0:1]),
                         reads=["ssB%d" % r, "epsb"], writes=["rsB%d" % r])
                    S.op("act", f_act(rsB[r][:], rsB[r][:], AF.Exp, scale=-0.5), reads=["rsB%d" % r], writes=["rsB%d" % r])
                    S.op("dve", f_stt(xrow, xrow, rsB[r][:, 0:1], gfb[:], ALU.mult, ALU.mult),
                         reads=["xtB%d" % sl, "rsB%d" % r, "gfb"], writes=["xtB%d" % sl])
                    yield
                S.op("sp", f_dma(ydst.rearrange("(s p) e -> p s e", p=128), xtB[sl][:, 0:nsub, :]),
                     reads=["xtB%d" % sl], writes=["y%d" % sl], dma="sty%d" % sl)
                yield
            return epi()

        for _ in prep(0):
            pass
        prev_epi = iter(())
        for T in range(NT_B):
            nxt = prep(T + 1) if T + 1 < NT_B else iter(())
            prev_epi = mlp(T, itertools.chain(prev_epi, nxt))
        for _ in prev_epi:
            pass

    if CFG["phaseB"]:
        phase_b()

    doms = S.domains()
    from contextlib import ExitStack
    with ExitStack() as es:
        sems = {}
        for dm in doms:
            sems[dm] = es.enter_context(nc.semaphore("s_%s_%s" % dm))
        with nc.Block() as block:
            S.emit(sems, block)
    return nc


def kernel(x_prompt, x_sample, cache_kv_w128, cache_kv_w512, cache_kv_w2048, state_conv,
           norm_attn_g, w_in, conv_w, w_out, norm_mlp_g, w_up, w_down, norm_final_g):
    f = lambda a: np.ascontiguousarray(np.asarray(a, dtype=np.float32))
    x_prompt, x_sample = f(x_prompt), f(x_sample)
    caches = [f(cache_kv_w128), f(cache_kv_w512), f(cache_kv_w2048)]
    state_conv = f(state_conv)
    nc = build_nc()
    shared = {
        "w_in": f(w_in)[0], "w_out": f(w_out)[0], "w_up": f(w_up)[0], "w_down": f(w_down)[0],
        "g_attn": f(norm_attn_g)[0], "g_mlp": f(norm_mlp_g)[0], "g_fin": f(norm_final_g),
        "conv_w": f(conv_w)[0],
    }
    in_maps = []
    for c in range(NCORES):
        m = dict(shared)
        m["xp"] = x_prompt[c]
        m["xs"] = x_sample[c * NSB:(c + 1) * NSB].reshape(NTS, D)
        for g in range(3):
            m["ck%d" % g] = caches[g][0, c * NSB:(c + 1) * NSB].reshape(NSB, WIN[g], 512)
        m["sconv"] = state_conv[0, c * NSB:(c + 1) * NSB]
        in_maps.append(m)
    res = run_bass_kernel_spmd(nc, in_maps, core_ids=list(range(NCORES)))
    R = res.results
    y_prompt = np.stack([R[c]["yp"] for c in range(NCORES)], 0)
    y_sample = np.concatenate([R[c]["ys"].reshape(NSB, 8, D) for c in range(NCORES)], 0)
    outs = [y_prompt, y_sample]
    for g in range(3):
        outs.append(np.stack([R[c]["kvp%d" % g].reshape(WIN[g], 2, 4, 64) for c in range(NCORES)], 0)[None])
    outs.append(np.stack([R[c]["convp"] for c in range(NCORES)], 0)[None])
    for g in range(3):
        outs.append(np.concatenate([R[c]["kvs%d" % g].reshape(NSB, WIN[g], 2, 4, 64) for c in range(NCORES)], 0)[None])
    outs.append(np.concatenate([R[c]["convs"] for c in range(NCORES)], 0)[None])
    return tuple(np.ascontiguousarray(o, dtype=np.float32) for o in outs)
```

```python
import itertools
import numpy as np
import concourse.bass as bass
import concourse.mybir as mybir
from concourse.bass_utils import run_bass_kernel_spmd

F32 = mybir.dt.float32
BF16 = mybir.dt.bfloat16
AF = mybir.ActivationFunctionType
ALU = mybir.AluOpType

NCORES = 8
D = 1024
SEQ = 8192
NSB = 16
NTS = 128
NTOK = SEQ + NTS
EPS = 1e-6
DIL = (1, 4, 16)
VW = 72
WIN = (128, 512, 2048)
ENGS = ("pe", "act", "dve", "pool", "sp")
CFG = {"d2d": 1, "nst": 4, "sample": 1, "phaseB": 1, "ntb": 33}


class Sched:
    def __init__(self):
        self.ops = []
        self.res = {}
        self.bar = {}

    def op(self, eng, fn, reads=(), writes=(), dma=None):
        idx = len(self.ops)
        ops = self.ops
        dom = ("dma", dma) if dma is not None else ("eng", eng)
        need = {}

        def add(j, kind):
            od = ops[j]["dom"]
            if dma is None and od == dom:
                if eng == "pe" or kind != "raw":
                    return
            if dma is not None and od == dom and kind == "waw":
                return
            if need.get(od, -1) < j:
                need[od] = j

        for r in reads:
            st = self.res.get(r)
            if st is None:
                st = self.res[r] = [[], [], []]
            for j in st[0]:
                add(j, "raw")
        for w in writes:
            st = self.res.get(w)
            if st is None:
                st = self.res[w] = [[], [], []]
            if st[1]:
                for j in st[1]:
                    add(j, "war")
                for j in st[0]:
                    add(j, "waw")
            else:
                for j in st[2]:
                    add(j, "war")
        for r in reads:
            self.res[r][1].append(idx)
        for w in writes:
            st = self.res[w]
            if st[1]:
                st[0] = [idx]
                st[2] = st[1]
                st[1] = []
            else:
                st[0].append(idx)
        for od, j in self.bar.items():
            if dma is None and od == dom:
                continue
            if need.get(od, -1) < j:
                need[od] = j
        keep = list(need.values())
        ops.append(dict(eng=eng, fn=fn, dom=dom, deps=keep, sig=(dma is not None), cnt=None))
        for j in keep:
            ops[j]["sig"] = True
        return idx

    def barrier(self):
        last = {}
        for i, o in enumerate(self.ops):
            last[o["dom"]] = i
        self.bar = last
        for j in last.values():
            self.ops[j]["sig"] = True

    def domains(self):
        return sorted(set(o["dom"] for o in self.ops))

    def emit(self, sems, block):
        cnt = {}
        for o in self.ops:
            if o["sig"]:
                inc = 16 if o["dom"][0] == "dma" else 1
                cnt[o["dom"]] = cnt.get(o["dom"], 0) + inc
                o["cnt"] = cnt[o["dom"]]
        per_eng = {e: [] for e in ENGS}
        for o in self.ops:
            per_eng[o["eng"]].append(o)
        ops = self.ops

        def run(engname, e):
            seen = {}
            mydom = ("eng", engname)
            for o in per_eng[engname]:
                for j in o["deps"]:
                    d = ops[j]
                    dom, v = d["dom"], d["cnt"]
                    if seen.get(dom, 0) >= v:
                        continue
                    e.wait_ge(sems[dom], v)
                    seen[dom] = v
                ins = o["fn"](e)
                if o["sig"]:
                    ins.then_inc(sems[o["dom"]], 16 if o["dom"][0] == "dma" else 1)
            last = {}
            for o in per_eng[engname]:
                if o["dom"][0] == "dma":
                    last[o["dom"]] = o["cnt"]
            for dom, v in last.items():
                if seen.get(dom, 0) < v:
                    e.wait_ge(sems[dom], v)

        block.tensor(lambda e: run("pe", e))
        block.scalar(lambda e: run("act", e))
        block.vector(lambda e: run("dve", e))
        block.gpsimd(lambda e: run("pool", e))
        block.sync(lambda e: run("sp", e))


class Mem:
    def __init__(self, nc, base=16384, limit=16384 + 212800):
        self.nc, self.off, self.limit = nc, base, limit
        self.n = 0
        self.offs = {}

    def alloc(self, name, shape, dt, at=None):
        nb = int(np.prod(shape[1:])) * (4 if dt == F32 else 2)
        nb = (nb + 63) // 64 * 64
        off = self.off if at is None else at
        t = self.nc.alloc_sbuf_tensor_at("%s_%d" % (name, self.n), list(shape), dt, offset=off)
        self.offs[name] = off
        self.n += 1
        if at is None:
            self.off += nb
            assert self.off <= self.limit, (name, self.off)
        return t


def f_mm(out, lhsT, rhs, start, stop, skip=False):
    if skip:
        return lambda e: e.matmul(out, lhsT=lhsT, rhs=rhs, start=start, stop=stop, skip_group_check=True)
    return lambda e: e.matmul(out, lhsT=lhsT, rhs=rhs, start=start, stop=stop)


def f_tr(out, in_, ident):
    return lambda e: e.transpose(out=out, in_=in_, identity=ident)


def f_act(out, in_, func, scale=None, bias=None, accum_out=None):
    kw = {}
    if scale is not None:
        kw["scale"] = scale
    if bias is not None:
        kw["bias"] = bias
    if accum_out is not None:
        kw["accum_out"] = accum_out
    return lambda e: e.activation(out=out, in_=in_, func=func, **kw)


def f_copy(out, in_):
    return lambda e: e.tensor_copy(out=out, in_=in_)


def f_acopy(out, in_):
    return lambda e: e.activation(out=out, in_=in_, func=AF.Copy)


def f_tt(out, in0, in1, op):
    return lambda e: e.tensor_tensor(out=out, in0=in0, in1=in1, op=op)


def f_ts(out, in0, s1, op0, s2=None, op1=None):
    if op1 is None:
        return lambda e: e.tensor_scalar(out=out, in0=in0, scalar1=s1, scalar2=None, op0=op0)
    return lambda e: e.tensor_scalar(out=out, in0=in0, scalar1=s1, scalar2=s2, op0=op0, op1=op1)


def f_stt(out, in0, scalar, in1, op0, op1):
    return lambda e: e.scalar_tensor_tensor(out=out, in0=in0, scalar=scalar, in1=in1, op0=op0, op1=op1)


def f_recip(out, in_):
    return lambda e: e.reciprocal(out=out, in_=in_)


def f_memset(ap, v):
    return lambda e: e.memset(ap, v)


def f_dma(out, in_, slow=False):
    if slow:
        return lambda e: e.dma_start(out=out, in_=in_, allow_slow_non_contiguous=True)
    return lambda e: e.dma_start(out=out, in_=in_)


def f_asel(out, in_, pattern, cmp, fill, base, cm):
    return lambda e: e.affine_select(out=out, in_=in_, pattern=pattern, compare_op=cmp, fill=fill,
                                     base=base, channel_multiplier=cm)


def build_nc():
    nc = bass.Bass("TRN2", target_bir_lowering=False)
    dr = lambda n, s, k: nc.dram_tensor(n, list(s), F32, kind=k).ap()
    xp = dr("xp", [SEQ, D], "ExternalInput")
    xs = dr("xs", [NTS, D], "ExternalInput")
    ck = [dr("ck%d" % g, [NSB, WIN[g], 512], "ExternalInput") for g in range(3)]
    sconv = dr("sconv", [NSB, 2, 256], "ExternalInput")
    w_in = dr("w_in", [D, 3072], "ExternalInput")
    w_out = dr("w_out", [D, D], "ExternalInput")
    w_up = dr("w_up", [D, 4096], "ExternalInput")
    w_down = dr("w_down", [4096, D], "ExternalInput")
    g_attn = dr("g_attn", [D], "ExternalInput")
    g_mlp = dr("g_mlp", [D], "ExternalInput")
    g_fin = dr("g_fin", [D], "ExternalInput")
    conv_w = dr("conv_w", [3, 256], "ExternalInput")
    yp = dr("yp", [SEQ, D], "ExternalOutput")
    ys = dr("ys", [NTS, D], "ExternalOutput")
    kvp = [dr("kvp%d" % g, [WIN[g], 512], "ExternalOutput") for g in range(3)]
    convp = dr("convp", [2, 256], "ExternalOutput")
    kvs = [dr("kvs%d" % g, [NSB, WIN[g], 512], "ExternalOutput") for g in range(3)]
    convs = dr("convs", [NSB, 2, 256], "ExternalOutput")
    NZ = dr("NZ", [NTOK, 780], "Internal")
    cvT = nc.dram_tensor("cvT", [256, NTOK], BF16, kind="Internal").ap()

    S = Sched()
    mem = Mem(nc)

    pp = [nc.alloc_psum_tensor("pp%d" % i, [128, 1024], F32) for i in range(4)]

    def bank(k):
        return pp[k // 2][:, (k % 2) * 512:(k % 2 + 1) * 512]

    def bank_bf(k):
        return bank(k).bitcast(BF16)

    ident = mem.alloc("ident", [128, 128], BF16)
    mask2 = mem.alloc("mask2", [128, 2, 256], BF16)
    epsb = mem.alloc("epsb", [128, 1], F32)
    g_attn_sb = mem.alloc("g_attn_sb", [128, 8], F32)
    g_mlp_sb = mem.alloc("g_mlp_sb", [128, 8], F32)
    cw_sb = mem.alloc("cw_sb", [128, 2, 3], F32)
    tmpf = mem.alloc("tmpf", [128, 256], F32)
    common_end = mem.off

    S.op("pool", f_memset(epsb[:], EPS), writes=["epsb"])
    S.op("pool", f_memset(tmpf[:, 0:128], 0.0), writes=["tmpf"])
    S.op("pool", f_asel(tmpf[:, 0:128], tmpf[:, 0:128], [[-1, 128]], ALU.not_equal, 1.0, 0, 1),
         reads=["tmpf"], writes=["tmpf"])
    S.op("dve", f_copy(ident[:], tmpf[:, 0:128]), reads=["tmpf"], writes=["ident"])
    S.op("pool", f_memset(tmpf[:], 1.0), reads=["tmpf"], writes=["tmpf"])
    S.op("pool", f_asel(tmpf[:, 0:128], tmpf[:, 0:128], [[-1, 128]], ALU.is_ge, 0.0, 0, 1),
         reads=["tmpf"], writes=["tmpf"])
    S.op("pool", f_asel(tmpf[:, 128:256], tmpf[:, 128:256], [[1, 128]], ALU.is_ge, 0.0, 0, -1),
         reads=["tmpf"], writes=["tmpf"])
    S.op("dve", f_copy(mask2[:, 0, :], tmpf[:]), reads=["tmpf"], writes=["mask2"])
    S.op("dve", f_copy(mask2[:, 1, :], tmpf[:]), reads=["tmpf"], writes=["mask2"])
    S.op("sp", f_dma(g_attn_sb[:], g_attn.rearrange("(k p) -> p k", p=128), slow=True),
         writes=["g_attn_sb"], dma="ld_ga")
    S.op("sp", f_dma(g_mlp_sb[:], g_mlp.rearrange("(k p) -> p k", p=128), slow=True),
         writes=["g_mlp_sb"], dma="ld_gm")
    for cc_ in range(2):
        S.op("sp", f_dma(cw_sb[:, cc_, :], conv_w[:, cc_ * 128:(cc_ + 1) * 128].rearrange("j p -> p j"), slow=True),
             writes=["cw_sb"], dma="ld_cw")

    d2d_next = [0]

    def d2d_chunk():
        bb = d2d_next[0]
        if bb >= NSB or not CFG["d2d"]:
            return
        d2d_next[0] += 1
        for g in range(3):
            W = WIN[g]
            S.op("act", f_dma(kvs[g][bb, 0:W - 8, :], ck[g][bb, 8:W, :]), dma="d2d%d" % g)

    w_in_sb = mem.alloc("w_in_sb", [128, 8, 3072], BF16)
    xnT = mem.alloc("xnT", [128, 8, 2048], BF16)
    xt = [mem.alloc("xt%d" % i, [128, D], F32) for i in range(2)]
    xn = [mem.alloc("xn%d" % i, [128, D], BF16) for i in range(2)]
    ssb = [mem.alloc("ss%d" % i, [128, 1], F32) for i in range(2)]
    rstd = [mem.alloc("rstd%d" % i, [128, 1], F32) for i in range(2)]
    QT = mem.alloc("QT", [128, 2, 2048], BF16)
    KTc = mem.alloc("KTc", [128, 2, 2048], BF16)
    VT = mem.alloc("VT", [128, 2, 2048], BF16)
    Vc = mem.alloc("Vc", [128, 16, 4, VW], BF16)
    KTh = [mem.alloc("KTh%d" % g, [128, 2, DIL[g] * 128], BF16) for g in range(3)]
    Vh = [mem.alloc("Vh%d" % g, [128, DIL[g], 4, VW], BF16) for g in range(3)]
    NPT = 4
    PT = [mem.alloc("PT%d" % i, [128, 2, 256], BF16) for i in range(NPT)]
    osb = [mem.alloc("osb%d" % i, [128, 4, 65], F32) for i in range(2)]
    Bsb = [mem.alloc("Bsb%d" % i, [128, 512], F32) for i in range(2)]
    Csb = [mem.alloc("Csb%d" % i, [128, 512], F32) for i in range(2)]
    ubuf = [mem.alloc("ubuf%d" % i, [128, 514], F32) for i in range(2)]
    ybuf = [mem.alloc("ybuf%d" % i, [128, 512], F32) for i in range(2)]
    cvo = [mem.alloc("cvo%d" % i, [128, 512], BF16) for i in range(2)]
    kvst = [mem.alloc("kvst%d" % i, [128, 512], F32) for i in range(2)]
    sQT = [mem.alloc("sQT%d" % g, [128, 2, 128], BF16) for g in range(3)]
    sKT = [mem.alloc("sKT%d" % g, [128, 2, 128], BF16) for g in range(3)]
    sVT = [mem.alloc("sVT%d" % g, [128, 2, 128], BF16) for g in range(3)]
    sVn = [mem.alloc("sVn%d" % g, [128, 4, VW], BF16) for g in range(3)]
    NKB = 3
    Kb = [mem.alloc("Kb%d" % i, [128, 256], BF16) for i in range(NKB)]
    sKTc = [mem.alloc("sKTc%d" % i, [128, 2, 128], BF16) for i in range(NKB)]
    NVC = 5
    sVc = [mem.alloc("sVc%d" % i, [128, 4, VW], BF16) for i in range(NVC)]
    NTE = 3
    tmpE = [mem.alloc("tmpE%d" % i, [128, 4, 8], F32) for i in range(NTE)]
    NPF = 4
    PTf = [mem.alloc("PTf%d" % i, [128, 2, 2, 128], BF16) for i in range(NPF)]
    PTn = mem.alloc("PTn", [128, 2, 2, 128], BF16)
    smask = mem.alloc("smask", [128, 13, 8], F32)
    nmask = mem.alloc("nmask", [128, 3, 128], BF16)
    ubs = [mem.alloc("ubs%d" % i, [128, NSB, 10], F32) for i in range(2)]
    kvss = [mem.alloc("kvss%d" % g, [128, 512], F32) for g in range(3)]
    osbs = mem.alloc("osbs", [128, 3, 260], F32)
    assert mem.offs["osbs"] + 3072 - mem.offs["kvss0"] >= 8192
    xt.append(mem.alloc("xt2", [128, D], F32, at=mem.offs["kvss0"]))
    xt.append(mem.alloc("xt3", [128, D], F32, at=mem.offs["kvss0"] + 4096))
    NXT = 4
    xo = mem.offs["PT0"]
    assert mem.offs["kvst1"] + 2048 - xo >= 2048 + 8192 + 16384
    kvc = [mem.alloc("kvc0", [128, 1, 512], F32, at=xo),
           mem.alloc("kvc1", [128, 4, 512], F32, at=xo + 2048),
           mem.alloc("kvc2", [128, 8, 512], F32, at=xo + 2048 + 8192)]
    KVC_ALIAS = (["PT%d" % i for i in range(NPT)] + ["osb0", "osb1"] +
                 [n + str(i) for n in ("Bsb", "Csb", "ubuf", "ybuf", "cvo", "kvst") for i in range(2)])
    memB = Mem(nc, base=common_end)
    w_out_sb = memB.alloc("w_out_sb", [128, 8, 1024], BF16)
    w_up_sb = memB.alloc("w_up_sb", [128, 8, 4096], BF16)
    w_down_sb = memB.alloc("w_down_sb", [128, 32, 1024], BF16)
    assert memB.off <= mem.offs["PT0"], (memB.off, mem.offs["PT0"])
    WB_ALIAS = (["w_in"] + ["xnT%d" % i for i in range(4)] + [n + str(i) for n in ("xt", "xn", "ss", "rstd") for i in range(2)] + ["xt2", "xt3"] +
                ["QT", "KTc", "VT", "Vc"] + ["KTh%d" % g for g in range(3)] + ["Vh%d" % g for g in range(3)])

    def load_phaseB_weights():
        first = [True]

        def al():
            r = WB_ALIAS if first[0] else []
            first[0] = False
            return r
        w_out_v = w_out.rearrange("(k p) e -> p k e", p=128)
        for kc in range(8):
            S.op("pool", f_dma(w_out_sb[:, kc, :], w_out_v[:, kc, :]), writes=["w_out"] + al(), dma="ld_wo")
        w_up_v = w_up.rearrange("(k p) e -> p k e", p=128)
        for cb in range(8):
            S.op("pool", f_dma(w_up_sb[:, :, cb * 512:(cb + 1) * 512], w_up_v[:, :, cb * 512:(cb + 1) * 512]),
                 writes=["w_up%d" % cb], dma="ld_wu%d" % cb)
        w_down_v = w_down.rearrange("(c p) e -> p c e", p=128)
        for cb in range(8):
            S.op("pool", f_dma(w_down_sb[:, cb * 4:(cb + 1) * 4, :], w_down_v[:, cb * 4:(cb + 1) * 4, :]),
                 writes=["w_down%d" % cb], dma="ld_wd%d" % cb)

    w_in_v = w_in.rearrange("(k p) e -> p k e", p=128)
    for kc in range(8):
        S.op("pool", f_dma(w_in_sb[:, kc, :], w_in_v[:, kc, :]), writes=["w_in"], dma="ld_w_in")

    for cc_ in range(2):
        for j_ in range(2):
            S.op("sp", f_dma(ubs[cc_][:, :, j_], sconv[:, j_, cc_ * 128:(cc_ + 1) * 128].rearrange("b p -> p b"), slow=True),
                 writes=["ubs%d" % cc_], dma="ld_sc%d" % cc_)
    S.op("pool", f_memset(Vc[:, :, :, 64:VW], 1.0), writes=["Vc"])
    for g in range(3):
        S.op("pool", f_memset(Vh[g][:, :, :, 64:VW], 1.0), writes=["Vh%d" % g])
        S.op("pool", f_memset(sVn[g][:, :, 64:VW], 1.0), writes=["sVn%d" % g])
    for i in range(NVC):
        S.op("pool", f_memset(sVc[i][:, :, 64:VW], 1.0), writes=["sVc%d" % i])
    for i in range(2):
        S.op("pool", f_memset(ubuf[i][:, 0:2], 0.0), writes=["ubuf%d" % i])
    for i in range(NPF):
        S.op("pool", f_memset(PTf[i][:], 0.0), writes=["PTf%d" % i])
    S.op("pool", f_memset(smask[:], 0.0), writes=["smask"])
    S.op("pool", f_memset(smask[:, 0, :], 1.0), reads=["smask"], writes=["smask"])
    S.op("pool", f_asel(smask[:, 0, :], smask[:, 0, :], [[-1, 8]], ALU.is_ge, 0.0, 0, 1),
         reads=["smask"], writes=["smask"])
    for rho in range(4):
        S.op("pool", f_memset(smask[:, 1 + rho, rho:rho + 1], 1.0), reads=["smask"], writes=["smask"])
        S.op("pool", f_memset(smask[:, 1 + rho, rho + 4:rho + 5], 1.0), reads=["smask"], writes=["smask"])
        S.op("pool", f_memset(smask[0:1, 1 + rho, rho + 4:rho + 5], 0.0), reads=["smask"], writes=["smask"])
    for rho in range(8):
        S.op("pool", f_memset(smask[:, 5 + rho, rho:rho + 1], 1.0), reads=["smask"], writes=["smask"])
    blkpat = [[-8, 16], [0, 8]]
    t3 = lambda ap: ap.rearrange("p (b t) -> p b t", t=8)
    S.op("pool", f_memset(tmpf[:, 0:128], 1.0), reads=["tmpf"], writes=["tmpf"])
    S.op("pool", f_asel(tmpf[:, 0:128], tmpf[:, 0:128], [[1, 128]], ALU.is_ge, 0.0, 0, -1),
         reads=["tmpf"], writes=["tmpf"])
    S.op("pool", f_asel(t3(tmpf[:, 0:128]), t3(tmpf[:, 0:128]), blkpat, ALU.is_ge, 0.0, 0, 1),
         reads=["tmpf"], writes=["tmpf"])
    S.op("dve", f_copy(nmask[:, 0, :], tmpf[:, 0:128]), reads=["tmpf"], writes=["nmask"])
    S.op("pool", f_memset(tmpf[:, 0:128], 0.0), reads=["tmpf"], writes=["tmpf"])
    S.op("pool", f_asel(tmpf[:, 0:128], tmpf[:, 0:128], [[1, 128]], ALU.not_equal, 1.0, 0, -1),
         reads=["tmpf"], writes=["tmpf"])
    S.op("pool", f_asel(tmpf[:, 0:128], tmpf[:, 0:128], [[1, 128]], ALU.not_equal, 1.0, -4, -1),
         reads=["tmpf"], writes=["tmpf"])
    S.op("pool", f_asel(t3(tmpf[:, 0:128]), t3(tmpf[:, 0:128]), blkpat, ALU.is_ge, 0.0, 0, 1),
         reads=["tmpf"], writes=["tmpf"])
    S.op("dve", f_copy(nmask[:, 1, :], tmpf[:, 0:128]), reads=["tmpf"], writes=["nmask"])
    S.op("dve", f_copy(nmask[:, 2, :], ident[:]), reads=["ident"], writes=["nmask"])

    ROLE_P = (0, 1)
    ROLE_O = (6, 7)
    B_XT = 7
    B_VT = 7
    SPAIR = (pp[1], pp[2], pp[0])
    STOK = (["pairS0"], ["pairS1"], ["bank0", "bank1"])

    def spair(k):
        return SPAIR[k][:, :].rearrange("p (b x) -> p b x", b=2)
    cnt = {"x": 0, "p": 0, "ev": 0, "s": 0, "o": 0, "pt": 0, "os": 0, "kv": 0}
    bk = lambda b: "bank%d" % b

    def xtile_l(src_ap):
        i = cnt["x"]
        cnt["x"] += 1
        sl = i % NXT
        S.op("act", f_dma(xt[sl][:], src_ap), writes=["xt%d" % sl], dma="ldx%d" % sl)
        return sl

    def xtile_c(xs_):
        sl = xs_ % 2
        S.op("act", f_act(xn[sl][:], xt[xs_][:], AF.Square, accum_out=ssb[sl][:]),
             reads=["xt%d" % xs_], writes=["xn%d" % sl, "ss%d" % sl])
        S.op("act", f_act(rstd[sl][:], ssb[sl][:], AF.Ln, scale=1.0 / D, bias=epsb[:, 0:1]),
             reads=["ss%d" % sl, "epsb"], writes=["rstd%d" % sl])
        S.op("act", f_act(rstd[sl][:], rstd[sl][:], AF.Exp, scale=-0.5),
             reads=["rstd%d" % sl], writes=["rstd%d" % sl])
        S.op("dve", f_ts(xn[sl][:], xt[xs_][:], rstd[sl][:, 0:1], ALU.mult),
             reads=["xt%d" % xs_, "rstd%d" % sl, "xn%d" % sl], writes=["xn%d" % sl])

    def xtile_b(xs_, col0, xtok):
        sl = xs_ % 2
        psx = bank_bf(B_XT).rearrange("p (k t) -> p k t", k=8)
        for kc in range(8):
            S.op("pe", f_tr(psx[:, kc, :], xn[sl][:, kc * 128:(kc + 1) * 128], ident[:]),
                 reads=["xn%d" % sl, "ident"], writes=[bk(B_XT)])
        gb = g_attn_sb[:, :].unsqueeze(2).broadcast_to([128, 8, 128])
        S.op("dve", f_tt(xnT[:, :, col0:col0 + 128], psx, gb, ALU.mult),
             reads=[bk(B_XT), "g_attn_sb"], writes=[xtok])

    def xtile(src_ap, col0, xtok):
        sl = xtile_l(src_ap)
        xtile_c(sl)
        xtile_b(sl, col0, xtok)

    def xsteps(srcs):
        n = len(srcs)
        slot = {}

        def mk_l(k):
            def f():
                slot[k] = xtile_l(srcs[k][0])
            return f

        def mk_c(k):
            return lambda: xtile_c(slot[k])

        def mk_b(k):
            return lambda: xtile_b(slot[k], srcs[k][1], srcs[k][2])
        pre = [mk_l(k) for k in range(min(NXT, n))]
        groups = []
        for k in range(n):
            g_ = []
            if k + 1 < n:
                g_.append(mk_c(k + 1))
            g_.append(mk_b(k))
            if k + NXT < n:
                g_.append(mk_l(k + NXT))
            groups.append(g_)
        return pre, mk_c(0), groups

    def run_all_steps(steps):
        for f in steps:
            f()

    def run_group(grp):
        if grp is not None:
            for f in grp:
                f()

    def run_with(gen, groups):
        for _ in gen:
            run_group(next(groups, None))
        for grp in groups:
            run_group(grp)

    def evac(out_ap, in_ap, reads, writes):
        k = cnt["ev"]
        cnt["ev"] += 1
        if k % 2 == 0:
            S.op("act", f_acopy(out_ap, in_ap), reads=reads, writes=writes)
        else:
            S.op("dve", f_copy(out_ap, in_ap), reads=reads, writes=writes)

    def proj_chunk(col, ntok, tok0, xtoks):
        b = ROLE_P[cnt["p"] % 2]
        cnt["p"] += 1
        for kc in range(8):
            S.op("pe", f_mm(bank(b)[:, 0:ntok], w_in_sb[:, kc, col:col + 128], xnT[:, kc, tok0:tok0 + ntok],
                            kc == 0, kc == 7), reads=["w_in"] + xtoks, writes=[bk(b)])
        return b

    def tokmajor_kv(g, tok0, xtoks):
        b = ROLE_P[cnt["p"] % 2]
        cnt["p"] += 1
        for t in range(2):
            col = (1 + t) * 768 + g * 256
            for kc in range(8):
                S.op("pe", f_mm(bank(b)[:, t * 256:(t + 1) * 256], xnT[:, kc, tok0:tok0 + 128],
                                w_in_sb[:, kc, col:col + 256], kc == 0, kc == 7),
                     reads=["w_in"] + xtoks, writes=[bk(b)])
        return b

    def group_proj(g, ct, steps=None):
        d = DIL[g]
        m = 512 // d
        for t, dest, tok in ((0, QT, "QT"), (1, KTc, "KTc"), (2, VT, "VT")):
            for p in range(2):
                col = t * 768 + g * 256 + p * 128
                b = proj_chunk(col, 512, ct * 512, ["xnT%d" % ct])
                out_ap = dest[:, p, :].rearrange("p (r m) -> p r m", r=d)[:, :, ct * m:(ct + 1) * m]
                in_ap = bank(b).rearrange("p (m r) -> p r m", r=d)
                evac(out_ap, in_ap, [bk(b)], [tok])
                if steps is not None:
                    run_group(next(steps, None))

    def v_transposes(g):
        for r in range(8):
            b_ = (B_VT, ROLE_O[0])[r % 2]
            psv = bank_bf(b_)
            for c2 in range(2):
                c = 2 * r + c2
                for p in range(2):
                    S.op("pe", f_tr(psv[:, c2 * 256 + p * 128:c2 * 256 + (p + 1) * 128], VT[:, p, c * 128:(c + 1) * 128],
                                    ident[:]), reads=["VT", "ident"], writes=[bk(b_)])
            S.op("dve", f_copy(Vc[:, 2 * r:2 * r + 2, :, 0:64],
                               psv[:, 0:512].rearrange("p (c h e) -> p c h e", c=2, h=4)),
                 reads=[bk(b_)], writes=["Vc"])

    def attention(g, st, filler=None, every=3):
        d = DIL[g]
        nblk = 16 // d
        units = []
        for rho in range(d):
            for j in range(nblk):
                for p in range(2):
                    units.append((rho, j, p))
        LAG = 3
        state = {}

        def emit_S(u):
            rho, j, p = units[u]
            c = rho * nblk + j
            has_prev = not (st == 0 and j == 0)
            sb_ = cnt["s"] % 3
            cnt["s"] += 1
            ps = spair(sb_)
            stok = STOK[sb_]
            if has_prev:
                for hh in range(2):
                    rows = slice(hh * 64, hh * 64 + 64)
                    if j == 0:
                        kprev = KTh[g][rows, p, rho * 128:(rho + 1) * 128]
                        ktok = "KTh%d" % g
                    else:
                        kprev = KTc[rows, p, (c - 1) * 128:c * 128]
                        ktok = "KTc"
                    S.op("pe", f_mm(ps[:, hh, 0:128], kprev, QT[rows, p, c * 128:(c + 1) * 128], True, True),
                         reads=["QT", ktok], writes=stok)
            for hh in range(2):
                rows = slice(hh * 64, hh * 64 + 64)
                S.op("pe", f_mm(ps[:, hh, 128:256], KTc[rows, p, c * 128:(c + 1) * 128],
                                QT[rows, p, c * 128:(c + 1) * 128], True, True),
                     reads=["QT", "KTc"], writes=stok)
            pt = cnt["pt"] % NPT
            cnt["pt"] += 1
            lo = 0 if has_prev else 128
            if CFG.get("att", 9) >= 2:
                S.op("act", f_act(PT[pt][:, :, lo:256], ps[:, :, lo:256], AF.Exp, scale=0.125),
                     reads=stok, writes=["PT%d" % pt])
                S.op("dve", f_tt(PT[pt][:, :, lo:256], PT[pt][:, :, lo:256], mask2[:, :, lo:256], ALU.mult),
                     reads=["PT%d" % pt, "mask2"], writes=["PT%d" % pt])
            state[u] = (pt, has_prev)

        def emit_PV(u):
            rho, j, p = units[u]
            c = rho * nblk + j
            pt, has_prev = state.pop(u)
            if CFG.get("att", 9) < 4:
                return
            if p == 0:
                state["o"] = ROLE_O[cnt["o"] % len(ROLE_O)]
                cnt["o"] += 1
            ob = state["o"]
            po = bank(ob)[:, 0:260].rearrange("p (h e) -> p h e", h=4)
            for hh in range(2):
                h = 2 * p + hh
                if has_prev:
                    if j == 0:
                        vprev, vtok = Vh[g][:, rho, h, 0:65], "Vh%d" % g
                    else:
                        vprev, vtok = Vc[:, c - 1, h, 0:65], "Vc"
                    S.op("pe", f_mm(po[:, h, :], PT[pt][:, hh, 0:128], vprev, True, False),
                         reads=["PT%d" % pt, vtok], writes=[bk(ob)])
                S.op("pe", f_mm(po[:, h, :], PT[pt][:, hh, 128:256], Vc[:, c, h, 0:65], not has_prev, True),
                     reads=["PT%d" % pt, "Vc"], writes=[bk(ob)])
            if p == 1 and CFG.get("att", 9) >= 5:
                os_ = cnt["os"] % 2
                cnt["os"] += 1
                S.op("dve", f_copy(osb[os_][:], po), reads=[bk(ob)], writes=["osb%d" % os_])
                n0 = (st * 2048) // d + 128 * j
                dst = NZ.rearrange("(n r) c -> r n c", r=d)[rho, n0:n0 + 128, g * 260:(g + 1) * 260]
                S.op("sp", f_dma(dst, osb[os_][:].rearrange("p h e -> p (h e)")),
                     reads=["osb%d" % os_], writes=["NZ_st%d" % st], dma="stnz%d" % os_)

        n = len(units)
        for u in range(n + LAG):
            if u < n:
                emit_S(u)
            if u - LAG >= 0:
                emit_PV(u - LAG)
            if filler is not None and u % every == every - 1:
                next(filler, None)
        if filler is not None:
            for _ in filler:
                pass
        if st < 3:
            S.op("dve", f_copy(KTh[g][:].rearrange("p q (r m) -> p q r m", r=d),
                                KTc[:].rearrange("p q (r m) -> p q r m", r=d)[:, :, :, (nblk - 1) * 128:nblk * 128]),
                 reads=["KTc"], writes=["KTh%d" % g])
            S.op("act", f_acopy(Vh[g][:].rearrange("p r h e -> p r (h e)"),
                                Vc[:].rearrange("p (r j) h e -> p r j (h e)", r=d)[:, :, nblk - 1, :]),
                 reads=["Vc"], writes=["Vh%d" % g])

    def conv_pass(st, ct):
        tok0 = st * 2048 + ct * 512
        xtk = ["xnT%d" % ct]
        for cc in range(2):
            bB = proj_chunk(2304 + cc * 128, 512, ct * 512, xtk)
            S.op("act", f_acopy(Bsb[cc][:], bank(bB)), reads=[bk(bB)], writes=["Bsb%d" % cc])
            yield
            bC = proj_chunk(2560 + cc * 128, 512, ct * 512, xtk)
            S.op("act", f_acopy(Csb[cc][:], bank(bC)), reads=[bk(bC)], writes=["Csb%d" % cc])
            yield
            bH = proj_chunk(2816 + cc * 128, 512, ct * 512, xtk)
            u = ubuf[cc]
            S.op("dve", f_tt(u[:, 2:514], bank(bH), Csb[cc][:], ALU.mult),
                 reads=[bk(bH), "Csb%d" % cc], writes=["ubuf%d" % cc])
            y = ybuf[cc]
            S.op("dve", f_ts(y[:], u[:, 0:512], cw_sb[:, cc, 0:1], ALU.mult),
                 reads=["ubuf%d" % cc, "cw_sb"], writes=["ybuf%d" % cc])
            S.op("dve", f_stt(y[:], u[:, 1:513], cw_sb[:, cc, 1:2], y[:], ALU.mult, ALU.add),
                 reads=["ubuf%d" % cc, "cw_sb", "ybuf%d" % cc], writes=["ybuf%d" % cc])
            S.op("dve", f_stt(y[:], u[:, 2:514], cw_sb[:, cc, 2:3], y[:], ALU.mult, ALU.add),
                 reads=["ubuf%d" % cc, "cw_sb", "ybuf%d" % cc], writes=["ybuf%d" % cc])
            S.op("dve", f_tt(cvo[cc][:], Bsb[cc][:], y[:], ALU.mult),
                 reads=["Bsb%d" % cc, "ybuf%d" % cc], writes=["cvo%d" % cc])
            S.op("sp", f_dma(cvT[cc * 128:(cc + 1) * 128, tok0:tok0 + 512], cvo[cc][:]),
                 reads=["cvo%d" % cc], writes=["cvT_st%d" % st], dma="stcv%d" % cc)
            if st == 3 and ct == 3:
                S.op("sp", f_dma(convp.rearrange("j c -> c j")[cc * 128:(cc + 1) * 128, :], u[:, 512:514], slow=True),
                     reads=["ubuf%d" % cc], dma="stcp")
            S.op("dve", f_copy(u[:, 0:2], u[:, 512:514]), reads=["ubuf%d" % cc], writes=["ubuf%d" % cc])
            yield

    def conv_gen(st, cts):
        for ct in cts:
            yield from conv_pass(st, ct)

    def xt_gen(st):
        if st < 3:
            for ct in range(4):
                for s4 in range(4):
                    sub = ct * 4 + s4
                    t0 = (st + 1) * 2048 + sub * 128
                    xtile(xp[t0:t0 + 128, :], sub * 128, "xnT%d" % ct)
                    yield
        else:
            xtile(xs[:, :], 0, "xnT0")
            yield

    def kv_tail():
        for sub in range(16):
            for g in range(3):
                first_sub = 16 - WIN[g] // 128
                if sub < first_sub:
                    continue
                b = tokmajor_kv(g, sub * 128, ["xnT%d" % (sub // 4)])
                sl = cnt["kv"] % 2
                cnt["kv"] += 1
                evac(kvst[sl][:], bank(b), [bk(b)], ["kvst%d" % sl])
                r0 = (sub - first_sub) * 128
                S.op("sp", f_dma(kvp[g][r0:r0 + 128, :], kvst[sl][:]), reads=["kvst%d" % sl], dma="stkv%d" % sl)

    def xtiles_prompt(st, ct):
        for s4 in range(4):
            sub = ct * 4 + s4
            t0 = st * 2048 + sub * 128
            xtile(xp[t0:t0 + 128, :], sub * 128, "xnT%d" % ct)

    srcs0 = [(xp[sub * 128:(sub + 1) * 128, :], sub * 128, "xnT%d" % (sub // 4)) for sub in range(16)]
    pre0, c00, groups0 = xsteps(srcs0)
    run_all_steps(iter(pre0))
    c00()
    for grp in groups0[0:4]:
        run_group(grp)
    pending0 = iter(groups0[4:16])
    def run_all(gen):
        for _ in gen:
            pass

    pending = pending0
    for st in range(CFG["nst"]):
        if st < 3:
            srcs = []
            for sub in range(16):
                t0 = (st + 1) * 2048 + sub * 128
                srcs.append((xp[t0:t0 + 128, :], sub * 128, "xnT%d" % (sub // 4)))
        else:
            srcs = [(xs[:, :], 0, "xnT0")]
        pre, c0, groups = xsteps(srcs)
        for g in range(3):
            for ct in range(4):
                group_proj(g, ct, itertools.islice(pending, 4) if g == 0 else None)
            if g == 0:
                for grp in pending:
                    run_group(grp)
            v_transposes(g)
            if g == 2:
                run_all_steps(iter(pre))
            attention(g, st)
            d2d_chunk()
        if st == 3:
            kv_tail()
        cg = conv_pass(st, 0)
        next(cg)
        c0()
        for _ in cg:
            pass
        if st < 3:
            run_with(conv_pass(st, 1), iter(groups[0:4]))
            run_with(conv_pass(st, 2), iter(groups[4:8]))
            run_with(conv_pass(st, 3), iter(groups[8:12]))
            pending = iter(groups[12:16])
        else:
            run_with(conv_pass(st, 1), iter(groups))
            for _ in conv_pass(st, 2):
                pass
            for _ in conv_pass(st, 3):
                pass
            pending = iter(())
        d2d_chunk()
    while d2d_next[0] < NSB and CFG["d2d"]:
        d2d_chunk()

    def sample_phase():
        XS = ["xnT0"]
        for g in range(3):
            for t, dest, tok in ((0, sQT, "sQT"), (1, sKT, "sKT"), (2, sVT, "sVT")):
                for p in range(2):
                    col = t * 768 + g * 256 + p * 128
                    b = proj_chunk(col, 128, 0, XS)
                    evac(dest[g][:, p, :], bank(b)[:, 0:128], [bk(b)], ["%s%d" % (tok, g)])
        for g in range(3):
            b = tokmajor_kv(g, 0, XS)
            evac(kvss[g][:], bank(b), [bk(b)], ["kvss%d" % g, "xt2", "xt3"])
            W = WIN[g]
            for bb in range(NSB):
                S.op("sp", f_dma(kvs[g][bb, W - 8:W, :], kvss[g][bb * 8:(bb + 1) * 8, :]),
                     reads=["kvss%d" % g], dma="stkvs")
        late = []
        for cc in range(2):
            bB = proj_chunk(2304 + cc * 128, 128, 0, XS)
            S.op("act", f_acopy(Bsb[cc][:, 0:128], bank(bB)[:, 0:128]), reads=[bk(bB)], writes=["Bsb%d" % cc])
            bC = proj_chunk(2560 + cc * 128, 128, 0, XS)
            S.op("act", f_acopy(Csb[cc][:, 0:128], bank(bC)[:, 0:128]), reads=[bk(bC)], writes=["Csb%d" % cc])
            bH = proj_chunk(2816 + cc * 128, 128, 0, XS)
            u = ubs[cc]
            S.op("dve", f_tt(u[:, :, 2:10], t3(bank(bH)[:, 0:128]), t3(Csb[cc][:, 0:128]), ALU.mult),
                 reads=[bk(bH), "Csb%d" % cc], writes=["ubs%d" % cc])
            y = t3(ybuf[cc][:, 0:128])
            S.op("pool", f_ts(y, u[:, :, 0:8], cw_sb[:, cc, 0:1], ALU.mult),
                 reads=["ubs%d" % cc, "cw_sb"], writes=["ybuf%d" % cc])
            S.op("dve", f_stt(y, u[:, :, 1:9], cw_sb[:, cc, 1:2], y, ALU.mult, ALU.add),
                 reads=["ubs%d" % cc, "cw_sb", "ybuf%d" % cc], writes=["ybuf%d" % cc])
            S.op("dve", f_stt(y, u[:, :, 2:10], cw_sb[:, cc, 2:3], y, ALU.mult, ALU.add),
                 reads=["ubs%d" % cc, "cw_sb", "ybuf%d" % cc], writes=["ybuf%d" % cc])
            S.op("dve", f_tt(cvo[cc][:, 0:128], Bsb[cc][:, 0:128], ybuf[cc][:, 0:128], ALU.mult),
                 reads=["Bsb%d" % cc, "ybuf%d" % cc], writes=["cvo%d" % cc])
            S.op("sp", f_dma(cvT[cc * 128:(cc + 1) * 128, SEQ:SEQ + 128], cvo[cc][:, 0:128]),
                 reads=["cvo%d" % cc], writes=["cvT_s"], dma="stcv%d" % cc)
            for j_ in range(2):
                late.append((convs[:, j_, cc * 128:(cc + 1) * 128].rearrange("b p -> p b"), u[:, :, 8 + j_], "ubs%d" % cc))

        if CFG["phaseB"]:
            load_phaseB_weights()
        ACC = (0, 1, 6)
        psA = [bank(ACC[g])[:, 0:260].rearrange("p (h e) -> p h e", h=4) for g in range(3)]
        for g in range(3):
            S.op("dve", f_memset(bank(ACC[g])[:, 0:260], 0.0), writes=[bk(ACC[g])])
        for g in range(3):
            psv = bank_bf(B_VT)
            for p in range(2):
                S.op("pe", f_tr(psv[:, p * 128:(p + 1) * 128], sVT[g][:, p, :], ident[:]),
                     reads=["sVT%d" % g, "ident"], writes=[bk(B_VT)])
            S.op("dve", f_copy(sVn[g][:, :, 0:64], psv[:, 0:256].rearrange("p (h e) -> p h e", h=4)),
                 reads=[bk(B_VT)], writes=["sVn%d" % g])
            sb_ = cnt["s"] % 2
            cnt["s"] += 1
            ps = spair(sb_)
            stok = "pairS%d" % sb_
            for h in range(4):
                p, hh = h // 2, h % 2
                rows = slice(hh * 64, hh * 64 + 64)
                S.op("pe", f_mm(ps[:, hh, p * 128:(p + 1) * 128], sKT[g][rows, p, :], sQT[g][rows, p, :], True, True),
                     reads=["sKT%d" % g, "sQT%d" % g], writes=[stok])
            S.op("act", f_act(PTn[:].rearrange("k a b q -> k a (b q)"), ps[:, :, 0:256], AF.Exp, scale=0.125),
                 reads=[stok], writes=["PTn"])
            S.op("dve", f_tt(PTn[:].rearrange("k a b q -> k (a b) q"), PTn[:].rearrange("k a b q -> k (a b) q"),
                              nmask[:, g, :].unsqueeze(1).broadcast_to([128, 4, 128]), ALU.mult),
                 reads=["PTn", "nmask"], writes=["PTn"])
            for h in range(4):
                p, hh = h // 2, h % 2
                S.op("pe", f_mm(psA[g][:, h, :], PTn[:, hh, p, :], sVn[g][:, h, 0:65], False, False, skip=True),
                     reads=["PTn", "sVn%d" % g, bk(ACC[g])], writes=[bk(ACC[g])])
        ncls = (1, 4, 8)
        moff = (0, 1, 5)
        units = [(bb, g, cl) for bb in range(NSB) for g in range(3) for cl in range(ncls[g])]
        pairs = [(bb, g) for bb in range(NSB) for g in range(3)]

        def load_pair(i):
            if i >= len(pairs):
                return
            bb, g = pairs[i]
            src = ck[g][bb].rearrange("(m r) c -> m r c", r=DIL[g])[:, 0:ncls[g], :]
            wr = ["kvc%d" % g] + (KVC_ALIAS if i < 3 else [])
            S.op("sp", f_dma(kvc[g][:], src), writes=wr, dma="ldkv%d" % g)

        load_pair(0)
        load_pair(1)

        def stA(u):
            bb, g, cl = units[u]
            if cl == 0:
                load_pair(bb * 3 + g + 2)
            ki, vi = u % NKB, u % NVC
            S.op("act", f_acopy(Kb[ki][:], kvc[g][:, cl, 0:256]), reads=["kvc%d" % g], writes=["Kb%d" % ki])
            S.op("dve", f_copy(sVc[vi][:, :, 0:64], kvc[g][:, cl, 256:512].rearrange("p (h e) -> p h e", h=4)),
                 reads=["kvc%d" % g], writes=["sVc%d" % vi])

        def stT(u):
            ki = u % NKB
            psv = bank_bf(B_VT)
            for p in range(2):
                S.op("pe", f_tr(psv[:, p * 128:(p + 1) * 128], Kb[ki][:, p * 128:(p + 1) * 128], ident[:]),
                     reads=["Kb%d" % ki, "ident"], writes=[bk(B_VT)])
            S.op("act", f_acopy(sKTc[ki][:].rearrange("p a k -> p (a k)"), psv[:, 0:256]),
                 reads=[bk(B_VT)], writes=["sKTc%d" % ki])

        def stS(u):
            bb, g, cl = units[u]
            ki, ei, fi = u % NKB, u % NTE, u % NPF
            sb_ = u % 2
            ps = spair(sb_)
            stok = "pairS%d" % sb_
            for h in range(4):
                p, hh = h // 2, h % 2
                rows = slice(hh * 64, hh * 64 + 64)
                S.op("pe", f_mm(ps[:, hh, p * 8:(p + 1) * 8], sKTc[ki][rows, p, :],
                                sQT[g][rows, p, bb * 8:(bb + 1) * 8], True, True),
                     reads=["sKTc%d" % ki, "sQT%d" % g], writes=[stok])
            S.op("act", f_act(tmpE[ei][:].rearrange("k (a b) t -> k a (b t)", a=2), ps[:, :, 0:16], AF.Exp, scale=0.125),
                 reads=[stok], writes=["tmpE%d" % ei])
            mk = smask[:, moff[g] + cl, :].unsqueeze(1).broadcast_to([128, 4, 8])
            S.op("dve", f_tt(PTf[fi][:].rearrange("k a b q -> k (a b) q")[:, :, bb * 8:(bb + 1) * 8], tmpE[ei][:], mk, ALU.mult),
                 reads=["tmpE%d" % ei, "smask", "PTf%d" % fi], writes=["PTf%d" % fi])

        def stP(u):
            bb, g, cl = units[u]
            vi, fi = u % NVC, u % NPF
            for h in range(4):
                p, hh = h // 2, h % 2
                S.op("pe", f_mm(psA[g][:, h, :], PTf[fi][:, hh, p, :], sVc[vi][:, h, 0:65], False, False, skip=True),
                     reads=["PTf%d" % fi, "sVc%d" % vi, bk(ACC[g])], writes=[bk(ACC[g])])
            S.op("dve", f_memset(PTf[fi][:].rearrange("k a b q -> k (a b) q")[:, :, bb * 8:(bb + 1) * 8], 0.0),
                 reads=["PTf%d" % fi], writes=["PTf%d" % fi])

        nu = len(units)
        for i in range(nu + 3):
            if i < nu:
                stA(i)
            if 0 <= i - 1 < nu:
                stT(i - 1)
            if 0 <= i - 2 < nu:
                stS(i - 2)
            if 0 <= i - 3 < nu:
                stP(i - 3)
        for dst_, src_, tok_ in late:
            S.op("sp", f_dma(dst_, src_, slow=True), reads=[tok_], dma="stcs")
        for g in range(3):
            S.op("dve", f_copy(osbs[:, g, :], bank(ACC[g])[:, 0:260]), reads=[bk(ACC[g])], writes=["osbs", "xt3"])
        S.op("sp", f_dma(NZ[SEQ:SEQ + 128, :], osbs[:].rearrange("p g e -> p (g e)")), reads=["osbs"],
             writes=["NZ_s"], dma="stnz0")

    if CFG["sample"]:
        sample_phase()

    def phase_b():
        S.barrier()
        xtB = [memB.alloc("xtB%d" % i, [128, 2, D], F32) for i in range(2)]
        NZt = [memB.alloc("NZt%d" % i, [128, 2, 780], F32) for i in range(2)]
        attn_n = [memB.alloc("attn_n%d" % i, [128, 768], BF16) for i in range(2)]
        attnT = [memB.alloc("attnT%d" % i, [128, 8, 128], BF16) for i in range(2)]
        xn1 = [memB.alloc("xn1_%d" % i, [128, D], BF16) for i in range(2)]
        xn1T = [memB.alloc("xn1T%d" % i, [128, 8, 256], BF16) for i in range(2)]
        hr = [memB.alloc("hr%d" % i, [128, 256], BF16) for i in range(2)]
        NH = 8
        h1T = [memB.alloc("h1T%d" % i, [128, 256], BF16) for i in range(NH)]
        gfb = memB.alloc("gfb", [128, D], F32)
        junkB = memB.alloc("junkB", [128, D], BF16)
        Zt = [memB.alloc("Zt%d" % i, [128, 4], F32) for i in range(2)]
        Zr = [memB.alloc("Zr%d" % i, [128, 4], F32) for i in range(2)]
        ssB = [memB.alloc("ssB%d" % i, [128, 1], F32) for i in range(4)]
        rsB = [memB.alloc("rsB%d" % i, [128, 1], F32) for i in range(4)]

        S.op("sp", f_dma(gfb[:], g_fin.partition_broadcast(128)), writes=["gfb"], dma="ld_gf")
        if not CFG["sample"]:
            load_phaseB_weights()

        psD = [pp[1], pp[2]]
        B_U = (0, 1)
        B_M = 6
        B_T = 7
        bc = {"a": 0, "r": 0}
        NT_B = CFG["ntb"]

        def tile_info(T):
            if T < 32:
                return 2, T * 256, xp[T * 256:(T + 1) * 256, :], yp[T * 256:(T + 1) * 256, :], ["NZ_st%d" % (T // 8)], ["cvT_st%d" % (T // 8)]
            return 1, SEQ, xs[:, :], ys[:, :], ["NZ_s"], ["cvT_s"]

        def prep(T):
            nsub, tok0, xsrc, ydst, nztok, cvtok = tile_info(T)
            sl = T % 2
            S.op("sp", f_dma(xtB[sl][:, 0:nsub, :], xsrc.rearrange("(s p) e -> p s e", p=128)),
                 writes=["xtB%d" % sl], dma="ldxB%d" % sl)
            S.op("sp", f_dma(NZt[sl][:, 0:nsub, :], NZ[tok0:tok0 + nsub * 128, :].rearrange("(s p) c -> p s c", p=128)),
                 reads=nztok, writes=["NZt%d" % sl], dma="ldnz%d" % sl)
            for s in range(nsub):
                a = bc["a"] % 2
                bc["a"] += 1
                S.op("sp", f_dma(attnT[a][:, 6:8, :],
                                 cvT[:, tok0 + s * 128:tok0 + (s + 1) * 128].rearrange("(c p) t -> p c t", p=128)),
                     reads=cvtok, writes=["attnT%d" % a], dma="ldcv%d" % a)
                nzv = NZt[sl][:, s, :].rearrange("p (g h e) -> p g h e", g=3, h=4)
                S.op("dve", f_tt(Zt[a][:], nzv[:, 0, :, 64], nzv[:, 1, :, 64], ALU.add),
                     reads=["NZt%d" % sl], writes=["Zt%d" % a])
                S.op("dve", f_tt(Zt[a][:], Zt[a][:], nzv[:, 2, :, 64], ALU.add),
                     reads=["NZt%d" % sl, "Zt%d" % a], writes=["Zt%d" % a])
                S.op("dve", f_recip(Zr[a][:], Zt[a][:]), reads=["Zt%d" % a], writes=["Zr%d" % a])
                zb = Zr[a][:, :].unsqueeze(1).unsqueeze(3).broadcast_to([128, 3, 4, 64])
                S.op("dve", f_tt(attn_n[a][:].rearrange("p (g h e) -> p g h e", g=3, h=4), nzv[:, :, :, 0:64], zb, ALU.mult),
                     reads=["NZt%d" % sl, "Zr%d" % a], writes=["attn_n%d" % a])
                yield
                pst = bank_bf(B_T).rearrange("p (k t) -> p k t", k=8)
                for kc in range(6):
                    S.op("pe", f_tr(pst[:, kc, :], attn_n[a][:, kc * 128:(kc + 1) * 128], ident[:]),
                         reads=["attn_n%d" % a, "ident"], writes=[bk(B_T)])
                S.op("act", f_acopy(attnT[a][:, 0:6, :], pst[:, 0:6, :]), reads=[bk(B_T)], writes=["attnT%d" % a])
                yield
                xrow = xtB[sl][:, s, :]
                for half in range(2):
                    for kc in range(8):
                        S.op("pe", f_mm(bank(B_M), attnT[a][:, kc, :], w_out_sb[:, kc, half * 512:(half + 1) * 512],
                                        kc == 0, kc == 7), reads=["attnT%d" % a, "w_out"], writes=[bk(B_M)])
                    S.op("dve", f_tt(xrow[:, half * 512:(half + 1) * 512], xrow[:, half * 512:(half + 1) * 512],
                                     bank(B_M), ALU.add), reads=[bk(B_M), "xtB%d" % sl], writes=["xtB%d" % sl])
                r = bc["r"] % 4
                bc["r"] += 1
                S.op("act", f_act(xn1[a][:], xrow, AF.Square, accum_out=ssB[r][:]),
                     reads=["xtB%d" % sl], writes=["xn1_%d" % a, "ssB%d" % r])
                S.op("act", f_act(rsB[r][:], ssB[r][:], AF.Ln, scale=1.0 / D, bias=epsb[:, 0:1]),
                     reads=["ssB%d" % r, "epsb"], writes=["rsB%d" % r])
                S.op("act", f_act(rsB[r][:], rsB[r][:], AF.Exp, scale=-0.5), reads=["rsB%d" % r], writes=["rsB%d" % r])
                S.op("dve", f_ts(xn1[a][:], xrow, rsB[r][:, 0:1], ALU.mult),
                     reads=["xtB%d" % sl, "rsB%d" % r, "xn1_%d" % a], writes=["xn1_%d" % a])
                yield
                for kc in range(8):
                    S.op("pe", f_tr(pst[:, kc, :], xn1[a][:, kc * 128:(kc + 1) * 128], ident[:]),
                         reads=["xn1_%d" % a, "ident"], writes=[bk(B_T)])
                gb = g_mlp_sb[:, :].unsqueeze(2).broadcast_to([128, 8, 128])
                S.op("dve", f_tt(xn1T[sl][:, :, s * 128:(s + 1) * 128], pst, gb, ALU.mult),
                     reads=[bk(B_T), "g_mlp_sb"], writes=["xn1T%d" % sl])
                yield

        def mlp(T, filler):
            nsub, tok0, xsrc, ydst, nztok, cvtok = tile_info(T)
            sl = T % 2
            N = nsub * 128
            LAGU = 4

            def up(c):
                ub = B_U[c % 2]
                for kc in range(8):
                    S.op("pe", f_mm(bank(ub)[:, 0:N], w_up_sb[:, kc, c * 128:(c + 1) * 128], xn1T[sl][:, kc, 0:N],
                                    kc == 0, kc == 7), reads=["w_up%d" % (c // 4), "xn1T%d" % sl], writes=[bk(ub)])
                S.op("act", f_act(hr[c % 2][:, 0:N], bank(ub)[:, 0:N], AF.Relu), reads=[bk(ub)], writes=["hr%d" % (c % 2)])
                S.op("dve", f_tt(h1T[c % NH][:, 0:N], hr[c % 2][:, 0:N], hr[c % 2][:, 0:N], ALU.mult),
                     reads=["hr%d" % (c % 2)], writes=["h1T%d" % (c % NH)])

            def down(c):
                for s in range(nsub):
                    for half in range(2):
                        S.op("pe", f_mm(psD[s][:, half * 512:(half + 1) * 512], h1T[c % NH][:, s * 128:(s + 1) * 128],
                                        w_down_sb[:, c, half * 512:(half + 1) * 512], c == 0, c == 31),
                             reads=["h1T%d" % (c % NH), "w_down%d" % (c // 4)], writes=["psD%d" % s])

            for c in range(32 + LAGU):
                if c < 32:
                    up(c)
                if c - LAGU >= 0:
                    down(c - LAGU)
                if c % 2 == 1:
                    next(filler, None)
            for _ in filler:
                pass
            for s in range(nsub):
                xrow = xtB[sl][:, s, :]
                for half in range(2):
                    S.op("dve", f_tt(xrow[:, half * 512:(half + 1) * 512], xrow[:, half * 512:(half + 1) * 512],
                                     psD[s][:, half * 512:(half + 1) * 512], ALU.add),
                         reads=["psD%d" % s, "xtB%d" % sl], writes=["xtB%d" % sl])

            def epi():
                for s in range(nsub):
                    xrow = xtB[sl][:, s, :]
                    r = bc["r"] % 4
                    bc["r"] += 1
                    S.op("act", f_act(junkB[:], xrow, AF.Square, accum_out=ssB[r][:]),
                         reads=["xtB%d" % sl], writes=["junkB", "ssB%d" % r])
                    S.op("act", f_act(rsB[r][:], ssB[r][:], AF.Ln, scale=1.0 / D, bias=epsb[:, 0:1]),
                         reads=["ssB%d" % r, "epsb"], writes=["rsB%d" % r])
                    S.op("act", f_act(rsB[r][:], rsB[r][:], AF.Exp, scale=-0.5), reads=["rsB%d" % r], writes=["rsB%d" % r])
                    S.op("dve", f_stt(xrow, xrow, rsB[r][:, 0:1], gfb[:], ALU.mult, ALU.mult),
                         reads=["xtB%d" % sl, "rsB%d" % r, "gfb"], writes=["xtB%d" % sl])
                    yield
                S.op("sp", f_dma(ydst.rearrange("(s p) e -> p s e", p=128), xtB[sl][:, 0:nsub, :]),
                     reads=["xtB%d" % sl], writes=["y%d" % sl], dma="sty%d" % sl)
                yield
            return epi()

        for _ in prep(0):
            pass
        prev_epi = iter(())
        for T in range(NT_B):
            nxt = prep(T + 1) if T + 1 < NT_B else iter(())
            prev_epi = mlp(T, itertools.chain(prev_epi, nxt))
        for _ in prev_epi:
            pass

    if CFG["phaseB"]:
        phase_b()

    doms = S.domains()
    from contextlib import ExitStack
    with ExitStack() as es:
        sems = {}
        for dm in doms:
            sems[dm] = es.enter_context(nc.semaphore("s_%s_%s" % dm))
        with nc.Block() as block:
            S.emit(sems, block)
    return nc


def kernel(x_prompt, x_sample, cache_kv_w128, cache_kv_w512, cache_kv_w2048, state_conv,
           norm_attn_g, w_in, conv_w, w_out, norm_mlp_g, w_up, w_down, norm_final_g):
    f = lambda a: np.ascontiguousarray(np.asarray(a, dtype=np.float32))
    x_prompt, x_sample = f(x_prompt), f(x_sample)
    caches = [f(cache_kv_w128), f(cache_kv_w512), f(cache_kv_w2048)]
    state_conv = f(state_conv)
    nc = build_nc()
    shared = {
        "w_in": f(w_in)[0], "w_out": f(w_out)[0], "w_up": f(w_up)[0], "w_down": f(w_down)[0],
        "g_attn": f(norm_attn_g)[0], "g_mlp": f(norm_mlp_g)[0], "g_fin": f(norm_final_g),
        "conv_w": f(conv_w)[0],
    }
    in_maps = []
    for c in range(NCORES):
        m = dict(shared)
        m["xp"] = x_prompt[c]
        m["xs"] = x_sample[c * NSB:(c + 1) * NSB].reshape(NTS, D)
        for g in range(3):
            m["ck%d" % g] = caches[g][0, c * NSB:(c + 1) * NSB].reshape(NSB, WIN[g], 512)
        m["sconv"] = state_conv[0, c * NSB:(c + 1) * NSB]
        in_maps.append(m)
    res = run_bass_kernel_spmd(nc, in_maps, core_ids=list(range(NCORES)))
    R = res.results
    y_prompt = np.stack([R[c]["yp"] for c in range(NCORES)], 0)
    y_sample = np.concatenate([R[c]["ys"].reshape(NSB, 8, D) for c in range(NCORES)], 0)
    outs = [y_prompt, y_sample]
    for g in range(3):
        outs.append(np.stack([R[c]["kvp%d" % g].reshape(WIN[g], 2, 4, 64) for c in range(NCORES)], 0)[None])
    outs.append(np.stack([R[c]["convp"] for c in range(NCORES)], 0)[None])
    for g in range(3):
        outs.append(np.concatenate([R[c]["kvs%d" % g].reshape(NSB, WIN[g], 2, 4, 64) for c in range(NCORES)], 0)[None])
    outs.append(np.concatenate([R[c]["convs"] for c in range(NCORES)], 0)[None])
    return tuple(np.ascontiguousarray(o, dtype=np.float32) for o in outs)
```
